# Optimizing a Trainium2 kernel written in Bass

```python
import math
import jax, jax.numpy as jnp
from jax import lax
import numpy as np

D_MODEL = 2048
BATCH = 1
SEQ = 16384
DEPTH = 2
DEC_BATCH = 32
DEC_SEQ = 32
PAST_LEN = 1024

CHUNK = 64
N_EVEN = (DEPTH + 1) // 2
N_ODD = DEPTH // 2
S5_WIDTH = D_MODEL // 2
S5_GROUP = 16
S5_GROUPS = S5_WIDTH // S5_GROUP
S5_STATE = 64
S5_BLOCK = 128
ATT_WIDTH = D_MODEL // 2
ATT_VDIM = 128
ATT_HEADS = ATT_WIDTH // ATT_VDIM
QK_DIM = ATT_VDIM // 2
Q_BLOCK = 128
CONV_DIM = D_MODEL
CONV_W = 31
EVEN_IN = 2 * S5_WIDTH + 4 * ATT_WIDTH
EPS = 1e-6
NEG_INF = -1e30

kernel_name = "streaming_s5_diffattn_conformer_step"


def rms_norm(x, g):
    xf = x.astype(jnp.float32)
    y = xf * lax.rsqrt(jnp.mean(xf * xf, axis=-1, keepdims=True) + EPS)
    return (y * g.astype(jnp.float32)).astype(x.dtype)


def layer_norm(x, g, b):
    xf = x.astype(jnp.float32)
    mu = jnp.mean(xf, axis=-1, keepdims=True)
    xc = xf - mu
    y = xc * lax.rsqrt(jnp.mean(xc * xc, axis=-1, keepdims=True) + EPS)
    return (y * g.astype(jnp.float32) + b.astype(jnp.float32)).astype(x.dtype)


def qk_norm(x, g):
    shp = x.shape
    xf = x.astype(jnp.float32).reshape(shp[:-1] + (2, QK_DIM))
    xf = xf * lax.rsqrt(jnp.mean(xf * xf, axis=-1, keepdims=True) + EPS)
    return (xf.reshape(shp) * g.astype(jnp.float32)).astype(x.dtype)


def alibi_slopes():
    return 2.0 ** (-8.0 * jnp.arange(1, ATT_HEADS + 1, dtype=jnp.float32) / ATT_HEADS)


def complex_combine(e1, e2):
    a1r, a1i, b1r, b1i = e1
    a2r, a2i, b2r, b2i = e2
    ar = a1r * a2r - a1i * a2i
    ai = a1r * a2i + a1i * a2r
    br = a2r * b1r - a2i * b1i + b2r
    bi = a2r * b1i + a2i * b1r + b2i
    return (ar, ai, br, bi)


def s5_scan(u, h0r, h0i, lam_re, lam_im, log_dt, b_re, b_im, c_re, c_im, d_skip):
    f32 = jnp.float32
    bsz, L, _ = u.shape
    uf = u.astype(f32).reshape(bsz, L, S5_GROUPS, S5_GROUP)
    lr, li = lam_re.astype(f32), lam_im.astype(f32)
    dt = jnp.exp(log_dt.astype(f32))[:, None]
    mag = jnp.exp(lr * dt)
    ar = mag * jnp.cos(li * dt)
    ai = mag * jnp.sin(li * dt)
    den = lr * lr + li * li
    zr = ((ar - 1.0) * lr + ai * li) / den
    zi = (ai * lr - (ar - 1.0) * li) / den
    br, bi = b_re.astype(f32), b_im.astype(f32)
    bbr = zr[..., None] * br - zi[..., None] * bi
    bbi = zr[..., None] * bi + zi[..., None] * br
    cr, ci = c_re.astype(f32), c_im.astype(f32)
    d = d_skip.astype(f32)
    blk = S5_BLOCK if L % S5_BLOCK == 0 else L
    nb = L // blk
    ub = uf.reshape(bsz, nb, blk, S5_GROUPS, S5_GROUP).swapaxes(0, 1)

    def step(carry, u_blk):
        hr, hi = carry
        xr_in = jnp.einsum('btgh,gph->btgp', u_blk, bbr)
        xi_in = jnp.einsum('btgh,gph->btgp', u_blk, bbi)
        a_r = jnp.broadcast_to(ar, xr_in.shape)
        a_i = jnp.broadcast_to(ai, xr_in.shape)
        A_r, A_i, X_r, X_i = lax.associative_scan(complex_combine, (a_r, a_i, xr_in, xi_in), axis=1)
        xr = A_r * hr[:, None] - A_i * hi[:, None] + X_r
        xi = A_r * hi[:, None] + A_i * hr[:, None] + X_i
        y = (jnp.einsum('btgp,ghp->btgh', xr, cr) - jnp.einsum('btgp,ghp->btgh', xi, ci)
             + d * u_blk)
        return (xr[:, -1], xi[:, -1]), y

    (hr, hi), yb = lax.scan(step, (h0r.astype(f32), h0i.astype(f32)), ub)
    y = yb.swapaxes(0, 1).reshape(bsz, L, S5_WIDTH).astype(u.dtype)
    return y, hr, hi


def diff_attn_core(q, k, v, q_pos, k_pos, lam):
    f32 = jnp.float32
    scale = QK_DIM ** -0.5
    q1, q2 = q[..., :QK_DIM], q[..., QK_DIM:]
    k1, k2 = k[..., :QK_DIM], k[..., QK_DIM:]
    dist = jnp.abs(q_pos[:, None] - k_pos[None, :]).astype(f32)
    vis = (k_pos[None, :] // CHUNK) <= (q_pos[:, None] // CHUNK)
    bias = jnp.where(vis[None], -alibi_slopes()[:, None, None] * dist[None], NEG_INF)
    s1 = jnp.einsum('bqhd,bkhd->bhqk', q1, k1).astype(f32) * scale + bias
    s2 = jnp.einsum('bqhd,bkhd->bhqk', q2, k2).astype(f32) * scale + bias
    w = (jax.nn.softmax(s1, axis=-1) - lam * jax.nn.softmax(s2, axis=-1)).astype(v.dtype)
    return jnp.einsum('bhqk,bkhd->bqhd', w, v)


def setup_inputs(seed: int = 0) -> dict:
    key = jax.random.key(seed)
    ks = iter(jax.random.split(key, 48))
    f32 = jnp.float32
    D = D_MODEL

    def nrm(shape, s=1.0):
        return s * jax.random.normal(next(ks), shape, f32)

    n_idx = jnp.arange(S5_STATE, dtype=f32)
    return {
        "x_prompt": nrm((BATCH, SEQ, D)),
        "x_sample": nrm((DEC_BATCH, DEC_SEQ, D)),
        "c_prompt": nrm((BATCH, D)),
        "c_sample": nrm((DEC_BATCH, D)),
        "cache_k": nrm((N_EVEN, DEC_BATCH, PAST_LEN, ATT_HEADS, ATT_VDIM)),
        "cache_v": nrm((N_EVEN, DEC_BATCH, PAST_LEN, ATT_HEADS, ATT_VDIM)),
        "state_s5_re": nrm((N_EVEN, DEC_BATCH, S5_GROUPS, S5_STATE), 0.1),
        "state_s5_im": nrm((N_EVEN, DEC_BATCH, S5_GROUPS, S5_STATE), 0.1),
        "state_conv": nrm((N_ODD, DEC_BATCH, CONV_W - 1, CONV_DIM), 0.5),
        "norm_g": 1.0 + nrm((DEPTH, D), 0.05),
        "w_ada": nrm((DEPTH, D, 3 * D), 0.5 * D ** -0.5),
        "b_ada": nrm((DEPTH, 3 * D), 0.02),
        "w_in_even": nrm((N_EVEN, D, EVEN_IN), D ** -0.5),
        "w_out_even": nrm((N_EVEN, S5_WIDTH + ATT_WIDTH, D), (S5_WIDTH + ATT_WIDTH) ** -0.5),
        "s5_lam_re": -0.5 + nrm((N_EVEN, S5_GROUPS, S5_STATE), 0.01),
        "s5_lam_im": math.pi * n_idx + nrm((N_EVEN, S5_GROUPS, S5_STATE), 0.01),
        "s5_log_dt": jax.random.uniform(next(ks), (N_EVEN, S5_GROUPS), f32, math.log(1e-3), math.log(1e-1)),
        "s5_b_re": nrm((N_EVEN, S5_GROUPS, S5_STATE, S5_GROUP), (2 * S5_GROUP) ** -0.5),
        "s5_b_im": nrm((N_EVEN, S5_GROUPS, S5_STATE, S5_GROUP), (2 * S5_GROUP) ** -0.5),
        "s5_c_re": nrm((N_EVEN, S5_GROUPS, S5_GROUP, S5_STATE), S5_STATE ** -0.5),
        "s5_c_im": nrm((N_EVEN, S5_GROUPS, S5_GROUP, S5_STATE), S5_STATE ** -0.5),
        "s5_d": nrm((N_EVEN, S5_GROUPS, S5_GROUP)),
        "s5_w_glu": nrm((N_EVEN, S5_WIDTH, 2 * S5_WIDTH), S5_WIDTH ** -0.5),
        "q_norm_g": 1.0 + nrm((N_EVEN, ATT_VDIM), 0.05),
        "k_norm_g": 1.0 + nrm((N_EVEN, ATT_VDIM), 0.05),
        "lam_q1": nrm((N_EVEN, QK_DIM), 0.1),
        "lam_k1": nrm((N_EVEN, QK_DIM), 0.1),
        "lam_q2": nrm((N_EVEN, QK_DIM), 0.1),
        "lam_k2": nrm((N_EVEN, QK_DIM), 0.1),
        "attn_out_g": 1.0 + nrm((N_EVEN, ATT_VDIM), 0.05),
        "w_in_odd": nrm((N_ODD, D, 3 * CONV_DIM), D ** -0.5),
        "conv_w": nrm((N_ODD, CONV_W, CONV_DIM), CONV_W ** -0.5),
        "conv_b": nrm((N_ODD, CONV_DIM), 0.02),
        "conv_ln_g": 1.0 + nrm((N_ODD, CONV_DIM), 0.05),
        "conv_ln_b": nrm((N_ODD, CONV_DIM), 0.02),
        "w_out_odd": nrm((N_ODD, CONV_DIM, D), CONV_DIM ** -0.5),
    }


def reference(x_prompt, x_sample, c_prompt, c_sample, cache_k, cache_v, state_s5_re, state_s5_im,
              state_conv, norm_g, w_ada, b_ada, w_in_even, w_out_even, s5_lam_re, s5_lam_im,
              s5_log_dt, s5_b_re, s5_b_im, s5_c_re, s5_c_im, s5_d, s5_w_glu, q_norm_g, k_norm_g,
              lam_q1, lam_k1, lam_q2, lam_k2, attn_out_g, w_in_odd, conv_w, conv_b, conv_ln_g,
              conv_ln_b, w_out_odd):
    f32 = jnp.float32
    past = cache_k.shape[2]

    def modulate(x, c, l):
        mod = jax.nn.silu(c) @ w_ada[l] + b_ada[l]
        shift, scale, gate = jnp.split(mod, 3, axis=-1)
        h = rms_norm(x, norm_g[l]) * (1.0 + scale[:, None]) + shift[:, None]
        return h, gate[:, None]

    def even_mixer(h, l, h0r, h0i, k_past, v_past, pos0):
        i = l // 2
        bsz, L, _ = h.shape
        sw, aw = S5_WIDTH, ATT_WIDTH
        proj = h @ w_in_even[i]
        u, zs, q, k, v, za = jnp.split(proj, [sw, 2 * sw, 2 * sw + aw, 2 * sw + 2 * aw, 2 * sw + 3 * aw], axis=-1)
        y, hr, hi = s5_scan(u, h0r, h0i, s5_lam_re[i], s5_lam_im[i], s5_log_dt[i], s5_b_re[i], s5_b_im[i],
                            s5_c_re[i], s5_c_im[i], s5_d[i])
        ga, gb = jnp.split(jax.nn.gelu(y) @ s5_w_glu[i], 2, axis=-1)
        s5_out = ga * jax.nn.sigmoid(gb) * jax.nn.silu(zs)
        q = qk_norm(q.reshape(bsz, L, ATT_HEADS, ATT_VDIM), q_norm_g[i])
        k = qk_norm(k.reshape(bsz, L, ATT_HEADS, ATT_VDIM), k_norm_g[i])
        v = v.reshape(bsz, L, ATT_HEADS, ATT_VDIM)
        lam_init = 0.8 - 0.6 * math.exp(-0.3 * l)
        lam = (jnp.exp(jnp.sum(lam_q1[i].astype(f32) * lam_k1[i].astype(f32)))
               - jnp.exp(jnp.sum(lam_q2[i].astype(f32) * lam_k2[i].astype(f32))) + lam_init)
        if k_past is None:
            nb = L // Q_BLOCK
            k_pos = jnp.arange(L)
            qb = q.reshape(bsz, nb, Q_BLOCK, ATT_HEADS, ATT_VDIM).swapaxes(0, 1)
            pb = jnp.arange(L).reshape(nb, Q_BLOCK)
            ob = lax.map(lambda a: diff_attn_core(a[0], k, v, a[1], k_pos, lam), (qb, pb))
            o = ob.swapaxes(0, 1).reshape(bsz, L, ATT_HEADS, ATT_VDIM)
        else:
            k_all = jnp.concatenate([k_past.astype(k.dtype), k], axis=1)
            v_all = jnp.concatenate([v_past.astype(v.dtype), v], axis=1)
            k_pos = jnp.arange(k_all.shape[1])
            q_pos = pos0 + jnp.arange(L)
            o = diff_attn_core(q, k_all, v_all, q_pos, k_pos, lam)
        o = rms_norm(o, attn_out_g[i]) * (1.0 - lam_init)
        att_out = o.reshape(bsz, L, ATT_WIDTH) * jax.nn.silu(za)
        out = jnp.concatenate([s5_out, att_out], axis=-1) @ w_out_even[i]
        return out, k, v, hr, hi

    def odd_mixer(h, l, buf):
        i = l // 2
        a, b, z = jnp.split(h @ w_in_odd[i], 3, axis=-1)
        g = a * jax.nn.sigmoid(b)
        if buf is None:
            buf = jnp.zeros((h.shape[0], CONV_W - 1, CONV_DIM), g.dtype)
        xp = jnp.concatenate([buf.astype(g.dtype), g], axis=1)
        y = lax.conv_general_dilated(xp, conv_w[i][:, None, :].astype(xp.dtype), (1,), 'VALID',
                                     dimension_numbers=('NWC', 'WIO', 'NWC'),
                                     feature_group_count=CONV_DIM) + conv_b[i]
        y = jax.nn.silu(layer_norm(y, conv_ln_g[i], conv_ln_b[i])) * jax.nn.silu(z)
        return y @ w_out_odd[i], xp[:, -(CONV_W - 1):]

    yp, ys = x_prompt, x_sample
    kp, vp, srp, sip, cvp = [], [], [], [], []
    kn, vn, srs, sis, cvs = [], [], [], [], []
    for l in range(DEPTH):
        hp, gp = modulate(yp, c_prompt, l)
        hs, gs = modulate(ys, c_sample, l)
        if l % 2 == 0:
            i = l // 2
            z0 = jnp.zeros((yp.shape[0], S5_GROUPS, S5_STATE), f32)
            op, k_, v_, hr, hi = even_mixer(hp, l, z0, z0, None, None, 0)
            kp.append(k_); vp.append(v_); srp.append(hr); sip.append(hi)
            os_, k_, v_, hr, hi = even_mixer(hs, l, state_s5_re[i], state_s5_im[i], cache_k[i], cache_v[i], past)
            kn.append(k_); vn.append(v_); srs.append(hr); sis.append(hi)
        else:
            i = l // 2
            op, bp = odd_mixer(hp, l, None)
            cvp.append(bp)
            os_, bs = odd_mixer(hs, l, state_conv[i])
            cvs.append(bs)
        yp = yp + gp * op
        ys = ys + gs * os_

    return (yp, ys, jnp.stack(kp), jnp.stack(vp), jnp.stack(srp), jnp.stack(sip), jnp.stack(cvp),
            jnp.stack(kn), jnp.stack(vn), jnp.stack(srs), jnp.stack(sis), jnp.stack(cvs))
```

```python
import math
import contextlib
import numpy as np
import ml_dtypes
import concourse.bass as bass
import concourse.mybir as mybir
from concourse.bass_utils import run_bass_kernel_spmd

F32 = mybir.dt.float32
BF16 = mybir.dt.bfloat16
I32 = mybir.dt.int32
ALU = mybir.AluOpType
AF = mybir.ActivationFunctionType
AX = mybir.AxisListType

NCORES = 8
D = 2048
SEQ = 16384
NSEQ = 32
DSEQ = 32
PAST = 1024
NTOK = SEQ + NSEQ * DSEQ
NBLK = NTOK // 512
NPB = SEQ // 512
EPS = 1e-6
LAM_INIT0 = 0.8 - 0.6 * math.exp(-0.3 * 0)
NEG = -1.0e4
SCALE = 0.125
OWN = 2048 + 128


class Res:
    __slots__ = ("name", "w", "r", "wm")

    def __init__(self, name):
        self.name = name
        self.w = None
        self.r = {}
        self.wm = {}


class Tile:
    excl = False
    multi = False

    def __init__(self, h, name):
        self.h = h
        self.res = Res(name)

    def __getitem__(self, k):
        return self.h[k]

    def ap(self):
        return self.h.ap()


class Sched:
    ENG = ("pe", "act", "dve", "pool", "sp")

    def __init__(self, nc):
        self.nc = nc
        self.streams = {e: [] for e in self.ENG}
        self.cnt = {e: 0 for e in self.ENG}
        self.seen = {e: {} for e in self.ENG}
        self.semh = {}
        for e in ("pe", "act", "dve", "pool"):
            self.semh[e] = nc.alloc_semaphore(name="sem_" + e)
        self.dpool = {}
        self.dpos = {}
        for e, n in (("sp", 40), ("pool", 24), ("act", 12)):
            self.dpool[e] = []
            for i in range(n):
                k = "d_%s_%d" % (e, i)
                self.semh[k] = nc.alloc_semaphore(name=k)
                self.dpool[e].append([k, 0])
            self.dpos[e] = 0
        self.final = {}
        self.alias = {}

    def _inherit(self, t):
        t.res.r = dict(self.alias)
        return t

    def phase(self):
        return Phase(self)

    def sb(self, name, shape, dt):
        return self._inherit(Tile(self.nc.alloc_sbuf_tensor(name, list(shape), dt), name))

    def ps(self, name, shape, dt):
        t = Tile(self.nc.alloc_psum_tensor(name, list(shape), dt), name)
        t.excl = True
        return t

    def dram(self, name, shape, dt, kind="Internal", multi=True):
        t = Tile(self.nc.dram_tensor(name, list(shape), dt, kind=kind), name)
        t.multi = multi
        return t

    def _deps(self, e, reads, writes):
        deps = {}

        def add(tok):
            if tok is None:
                return
            k, v = tok
            if deps.get(k, 0) < v:
                deps[k] = v

        for r in reads:
            add(r.res.w)
            for k, v in r.res.wm.items():
                add((k, v))
        for w in writes:
            if not w.multi:
                add(w.res.w)
                for k, v in w.res.wm.items():
                    add((k, v))
            for k, v in w.res.r.items():
                add((k, v))
        waits = []
        for k, v in deps.items():
            if k == e and e == "pe":
                continue
            if self.seen[e].get(k, 0) < v:
                self.seen[e][k] = v
                waits.append((k, v))
        return waits

    def op(self, e, fn, reads=(), writes=()):
        if e != "pe":
            ex = [r for r in reads if r.excl]
            if ex:
                reads = [r for r in reads if not r.excl]
                writes = list(writes) + ex
        waits = self._deps(e, reads, writes)
        self.cnt[e] += 1
        seq = self.cnt[e]
        self.streams[e].append((waits, fn, (e, 1)))
        for r in reads:
            if r.res.r.get(e, 0) < seq:
                r.res.r[e] = seq
        for w in writes:
            if w.multi:
                w.res.wm[e] = seq
            else:
                w.res.w = (e, seq)
                w.res.wm = {}
                w.res.r = {}

    def dma(self, q, out, in_, reads=(), writes=(), fn=None, is_output=False):
        waits = self._deps(q, reads, writes)
        slot = self.dpool[q][self.dpos[q]]
        self.dpos[q] = (self.dpos[q] + 1) % len(self.dpool[q])
        k = slot[0]
        if slot[1] > 0 and self.seen[q].get(k, 0) < slot[1]:
            self.seen[q][k] = slot[1]
            waits.append((k, slot[1]))
        slot[1] += 16
        v = slot[1]
        if fn is None:
            fn = lambda e, o=out, i=in_: e.dma_start(out=o, in_=i)
        self.streams[q].append((waits, fn, (k, 16)))
        for r in reads:
            r.res.r[k] = v
        for w in writes:
            if w.multi:
                w.res.wm[k] = v
            else:
                w.res.w = (k, v)
                w.res.wm = {}
                w.res.r = {}
        self.final[k] = v

    def collective(self, fn, reads, writes):
        waits = self._deps("pool", reads, writes)
        self.ncc = getattr(self, "ncc", 0) + 1
        k = "cc%d" % self.ncc
        self.semh[k] = self.nc.alloc_semaphore(name=k)
        self.streams["pool"].append((waits, fn, (k, None)))
        for r in reads:
            r.res.r[k] = 1
        for w in writes:
            w.res.w = (k, 1)
            w.res.wm = {}
            w.res.r = {}

    def raw(self, e, fn):
        self.streams[e].append(([], fn, None))

    def finish(self):
        waits = [(k, v) for k, v in self.final.items()]
        self.streams["sp"].append((waits, None, None))

    def emit(self):
        nc = self.nc
        semh = self.semh

        def replay(e, stream):
            for waits, fn, inc in stream:
                for k, v in waits:
                    e.wait_ge(semh[k], v)
                if fn is None:
                    continue
                ins = fn(e)
                if inc is not None:
                    if inc[1] is None:
                        ins.then_inc(semh[inc[0]])
                    else:
                        ins.then_inc(semh[inc[0]], inc[1])

        with nc.Block() as block:
            @block.sync
            def _(e):
                replay(e, self.streams["sp"])

            @block.tensor
            def _(e):
                replay(e, self.streams["pe"])

            @block.scalar
            def _(e):
                replay(e, self.streams["act"])

            @block.vector
            def _(e):
                replay(e, self.streams["dve"])

            @block.gpsimd
            def _(e):
                replay(e, self.streams["pool"])


class Phase:
    def __init__(self, S):
        self.S = S
        self.stack = contextlib.ExitStack()
        self.tiles = []

    def sb(self, name, shape, dt):
        h = self.stack.enter_context(self.S.nc.sbuf_tensor(name, list(shape), dt))
        t = self.S._inherit(Tile(h, name))
        self.tiles.append(t)
        return t

    def close(self):
        al = self.S.alias
        for t in self.tiles:
            toks = list(t.res.r.items())
            if t.res.w is not None:
                toks.append(t.res.w)
            for k, v in toks:
                if al.get(k, 0) < v:
                    al[k] = v
        self.stack.close()


import os
DEBUG = False
KSTAGE = int(os.environ.get('KSTAGE', '99'))
KBLKS = os.environ.get('KBLKS', '')
KSA = int(os.environ.get('KSA', '9'))
KSAO = int(os.environ.get('KSAO', '0'))


def build_program(stage=99):
    nc = bass.Bass("TRN2", target_bir_lowering=False)
    S = Sched(nc)

    def din(name, shape, dt=F32):
        return Tile(nc.dram_tensor(name, list(shape), dt, kind="ExternalInput"), name)

    def dout(name, shape, dt=F32):
        t = Tile(nc.dram_tensor(name, list(shape), dt, kind="ExternalOutput"), name)
        t.multi = True
        return t

    x_all = din("x_all", [NTOK, D])
    c_all = din("c_all", [68, D])
    w_ada = din("w_ada", [2, D, 3 * D])
    b_ada = din("b_ada", [2, 3 * D])
    norm_g = din("norm_g", [2, D])
    wA = din("wA", [D, 640])
    qkg = din("qkg", [64, 4])
    sel = din("sel", [68, 128])
    s5pp = din("s5pp", [128, 4, 3])
    s5bc = din("s5bc", [3, 512])
    brT = din("brT", [128, 4, 128])
    biT = din("biT", [128, 4, 128])
    crT = din("crT", [128, 4, 128])
    ciT = din("ciT", [128, 4, 128])
    dpp = din("dpp", [128, 1])
    st_re = din("st_re", [32, 512])
    st_im = din("st_im", [32, 512])
    s5p_out = dout("s5p_out", [2, 4, 128])
    s5s_out = dout("s5s_out", [2, 32, 512])
    x_own = din("x_own", [OWN, D])
    x_halo = din("x_halo", [32, D])
    hmask = din("hmask", [128, 1])
    gidx = din("gidx", [128, 32], I32)
    w_zs = din("w_zs", [D, 1024])
    w_glu = din("w_glu", [1024, D])
    w_o0 = din("w_o0", [D, D])
    w_i1 = din("w_i1", [D, 3 * D])
    w_o1 = din("w_o1", [D, D])
    cvw = din("cvw", [128, 16, 31])
    cvv4 = din("cvv4", [128, 16, 3])
    cst = din("cst", [4, 30, D])
    y_out = dout("y_out", [OWN, D])
    convp_out = dout("convp_out", [30, D])
    convs_out = dout("convs_out", [4, 30, D])
    wz_d = S.dram("wz_d", [D, 1024], BF16)
    wg_d = S.dram("wg_d", [1024, D], BF16)
    wo0_d = S.dram("wo0_d", [D, D], BF16)
    wi1_d = S.dram("wi1_d", [D, 3 * D], BF16)
    wo1_d = S.dram("wo1_d", [D, D], BF16)
    kqpos = din("kqpos", [8, NTOK], BF16)
    kcpos = din("kcpos", [4, 1024], BF16)
    biasdiag_d = din("biasdiag", [128, 4, 512])
    biasnew_d = din("biasnew", [128, 256])
    lamv = din("lamv", [4, 64])
    aog = din("aog", [128, 1])
    ck = din("ck", [32, 1024, 128])
    cvv = din("cvv", [32, 1024, 128])
    kD = S.dram("kD", [2, 68, SEQ], BF16)
    vD = S.dram("vD", [SEQ, 128], BF16)
    EXR = 2 * 128 * 8
    EX_in = S.dram("EX_in", [EXR * 17, 128], BF16)
    EX_out = S.dram("EX_out", [8 * EXR * 17, 128], BF16, multi=False)
    k_out = dout("k_out", [NTOK, 128])
    if DEBUG:
        dbg_gy = dout("dbg_gy", [128, NTOK], BF16)
        dbg_att = dout("dbg_att", [128, NTOK], BF16)
        dbg_o = dout("dbg_o", [6, 128, 512], F32)
        dbg1 = dout("dbg1", [3, 128, 4, 512], F32)
        dbg2 = dout("dbg2", [2, 128, 4, 128], BF16)
        dbg3 = dout("dbg3", [34, 128, 4, 2], F32)
    v_out = dout("v_out", [NTOK, 128])

    ident_f = S.sb("ident_f", [128, 128], F32)
    ident_b = S.sb("ident_b", [128, 128], BF16)
    ones64 = S.sb("ones64", [64, 64], F32)
    S.op("pool", lambda e: e.memset(ident_f[:], 0.0), writes=[ident_f])
    S.op("pool", lambda e: e.affine_select(out=ident_f[:], in_=ident_f[:], pattern=[[-1, 128]],
                                            compare_op=ALU.not_equal, fill=1.0, base=0,
                                            channel_multiplier=1), reads=[ident_f], writes=[ident_f])
    S.op("pool", lambda e: e.tensor_copy(out=ident_b[:], in_=ident_f[:]), reads=[ident_f], writes=[ident_b])
    S.op("pool", lambda e: e.memset(ones64[:], 1.0 / 64.0), writes=[ones64])

    PS = [S.ps("psb%d" % i, [128, 512], F32) for i in range(8)]
    gen_rr = [0]

    def gen_bank(pool=(5, 6, 7)):
        b = pool[gen_rr[0] % len(pool)]
        gen_rr[0] += 1
        return PS[b]

    NR = 68
    ApT = [S.sb("ApT%d" % l, [128, 16], F32) for l in range(2)]
    SpT = [S.sb("SpT%d" % l, [128, 16], F32) for l in range(2)]
    AoT = [S.sb("AoT%d" % l, [128, 16, 4], F32) for l in range(2)]
    SoT = [S.sb("SoT%d" % l, [128, 16, 4], F32) for l in range(2)]
    gate_d = S.dram("gate_d", [2, 2, 128, D], F32)
    qkg_sb = S.sb("qkg_sb", [64, 4], F32)

    rotT = S.sb("rotT", [128, 4, 2], F32)
    rot1r = S.sb("rot1r", [128, 4, 2], F32)
    d_sb = S.sb("d_sb", [128, 1], F32)
    neg_lam = S.sb("neg_lam", [128, 1], F32)
    gsc = S.sb("gsc", [128, 1], F32)
    onesb = S.sb("onesb", [128, 128], BF16)
    ones128f = S.sb("ones128f", [128, 128], F32)
    S.op("pool", lambda e: e.memset(onesb[:], 1.0), writes=[onesb])
    S.op("pool", lambda e: e.memset(ones128f[:], 1.0 / 128.0), writes=[ones128f])
    PL = S.phase()
    AsT = PL.sb("AsT", [128, 16, 32], F32)
    SsT = PL.sb("SsT", [128, 16, 32], F32)
    wA_sb = PL.sb("wA_sb", [128, 16, 640], BF16)
    cosT = PL.sb("cosT", [128, 4, 512], F32)
    sinT = PL.sb("sinT", [128, 4, 512], F32)
    rdec = PL.sb("rdec", [128, 4, 512], F32)
    rdec_s = PL.sb("rdec_s", [128, 4, 512], F32)
    BreT = PL.sb("BreT", [128, 4, 128], BF16)
    BimT = PL.sb("BimT", [128, 4, 128], BF16)
    CreT = PL.sb("CreT", [128, 4, 128], BF16)
    CimT = PL.sb("CimT", [128, 4, 128], BF16)
    ginit_s = PL.sb("ginit_s", [128, 4, 2, 32], F32)
    P0 = S.phase()
    gate_p = [P0.sb("gate_p", [128, D], F32)] * 2
    gate_o = [P0.sb("gate_o", [128, D], F32)] * 2
    c_sb = P0.sb("c_sb", [NR, D], F32)
    S.dma("sp", c_sb[:], c_all.ap(), reads=[c_all], writes=[c_sb])
    sig_sb = P0.sb("sig_sb", [NR, D], F32)
    S.op("act", lambda e: e.activation(out=sig_sb[:], in_=c_sb[:], func=AF.Exp, scale=-1.0),
         reads=[c_sb], writes=[sig_sb])
    S.op("dve", lambda e: e.tensor_scalar(out=sig_sb[:], in0=sig_sb[:], scalar1=1.0, scalar2=None, op0=ALU.add),
         reads=[sig_sb], writes=[sig_sb])
    S.op("dve", lambda e: e.reciprocal(out=sig_sb[:], in_=sig_sb[:]), reads=[sig_sb], writes=[sig_sb])
    S.op("dve", lambda e: e.tensor_tensor(out=sig_sb[:], in0=sig_sb[:], in1=c_sb[:], op=ALU.mult),
         reads=[sig_sb, c_sb], writes=[sig_sb])
    scT = P0.sb("scT", [128, 16, NR], F32)
    for dt_ in range(16):
        pb = gen_bank()
        S.op("pe", lambda e, pb=pb, dt_=dt_: e.transpose(pb[:, 0:NR], sig_sb[0:NR, dt_ * 128:(dt_ + 1) * 128],
                                                         ident_f[0:NR, 0:NR]),
             reads=[sig_sb, ident_f], writes=[pb])
        S.op("dve", lambda e, pb=pb, dt_=dt_: e.tensor_copy(out=scT[:, dt_, :], in_=pb[:, 0:NR]),
             reads=[pb], writes=[scT])
    ones1 = P0.sb("ones1", [1, 128], F32)
    S.op("pool", lambda e: e.memset(ones1[:], 1.0), writes=[ones1])
    brow = [P0.sb("brow%d" % i, [1, 256], F32) for i in range(2)]
    sel4 = P0.sb("sel4", [NR, 128], F32)
    S.dma("sp", sel4[:], sel.ap(), reads=[sel], writes=[sel4])
    mod = P0.sb("mod", [NR, 3 * D], F32)
    g_sb = P0.sb("g_sb", [NR, D], F32)
    arow = P0.sb("arow", [NR, D], F32)
    wst = [P0.sb("wst%d" % i, [128, 16, 320], F32) for i in range(2)]
    wi = 0
    for l in range(2):
        for nb in range(24):
            wt = wst[wi % 2]
            br = brow[wi % 2]
            wi += 1
            S.dma("sp", wt[:, :, 0:256], w_ada.ap()[l, :, nb * 256:(nb + 1) * 256].rearrange("(kt p) n -> p kt n", p=128),
                  reads=[w_ada], writes=[wt])
            S.dma("sp", br[:], b_ada.ap()[l:l + 1, nb * 256:(nb + 1) * 256], reads=[b_ada], writes=[br])
            pb = gen_bank()
            for kt in range(16):
                S.op("pe", lambda e, pb=pb, wt=wt, kt=kt: e.matmul(pb[0:NR, 0:256], lhsT=scT[:, kt, :], rhs=wt[:, kt, 0:256],
                                                                   start=(kt == 0), stop=False),
                     reads=[scT, wt], writes=[pb])
            S.op("pe", lambda e, pb=pb, br=br: e.matmul(pb[0:NR, 0:256], lhsT=ones1[0:1, 0:NR], rhs=br[0:1, :],
                                                        start=False, stop=True),
                 reads=[ones1, br], writes=[pb])
            S.op("act", lambda e, pb=pb, nb=nb: e.activation(out=mod[:, nb * 256:(nb + 1) * 256], in_=pb[0:NR, 0:256],
                                                             func=AF.Identity),
                 reads=[pb], writes=[mod])
        S.dma("sp", g_sb[:], norm_g.ap()[l:l + 1, :].to_broadcast([NR, D]), reads=[norm_g], writes=[g_sb])
        S.op("dve", lambda e: e.scalar_tensor_tensor(out=arow[:], in0=mod[:, D:2 * D], scalar=1.0, in1=g_sb[:],
                                                     op0=ALU.add, op1=ALU.mult),
             reads=[mod, g_sb], writes=[arow])
        for src, c0, dsts in ((arow, 0, (AsT, ApT[l], AoT[l])), (mod, 0, (SsT, SpT[l], SoT[l]))):
            for dt_ in range(16):
                cs = slice(c0 + dt_ * 128, c0 + (dt_ + 1) * 128)
                if l == 0:
                    pb = gen_bank()
                    S.op("pe", lambda e, pb=pb, src=src, cs=cs: e.transpose(pb[:, 0:32], src[0:32, cs], ident_f[0:32, 0:32]),
                         reads=[src, ident_f], writes=[pb])
                    S.op("dve", lambda e, pb=pb, d=dsts[0], dt_=dt_: e.tensor_copy(out=d[:, dt_, :], in_=pb[:, 0:32]),
                         reads=[pb], writes=[dsts[0]])
                pb = gen_bank()
                S.op("pe", lambda e, pb=pb, src=src, cs=cs: e.transpose(pb[:, 0:1], src[32:33, cs], ident_f[32:33, 32:33]),
                     reads=[src, ident_f], writes=[pb])
                S.op("dve", lambda e, pb=pb, d=dsts[1], dt_=dt_: e.tensor_copy(out=d[:, dt_:dt_ + 1], in_=pb[:, 0:1]),
                     reads=[pb], writes=[dsts[1]])
                pb = gen_bank()
                S.op("pe", lambda e, pb=pb, src=src, cs=cs: e.transpose(pb[:, 0:4], src[64:68, cs], ident_f[64:68, 64:68]),
                     reads=[src, ident_f], writes=[pb])
                S.op("dve", lambda e, pb=pb, d=dsts[2], dt_=dt_: e.tensor_copy(out=d[:, dt_, :], in_=pb[:, 0:4]),
                     reads=[pb], writes=[dsts[2]])
        for nb in range(4):
            cs = slice(2 * D + nb * 512, 2 * D + (nb + 1) * 512)
            pb = gen_bank()
            S.op("pe", lambda e, pb=pb, cs=cs: e.matmul(pb[:, :], lhsT=sel4[32:33, :], rhs=mod[32:33, cs], start=True, stop=True),
                 reads=[sel4, mod], writes=[pb])
            S.op("act", lambda e, pb=pb, nb=nb, l=l: e.activation(out=gate_p[l][:, nb * 512:(nb + 1) * 512], in_=pb[:, :],
                                                                 func=AF.Identity), reads=[pb], writes=[gate_p[l]])
            pb = gen_bank()
            S.op("pe", lambda e, pb=pb, cs=cs: e.matmul(pb[:, :], lhsT=sel4[64:68, :], rhs=mod[64:68, cs], start=True, stop=True),
                 reads=[sel4, mod], writes=[pb])
            S.op("act", lambda e, pb=pb, nb=nb, l=l: e.activation(out=gate_o[l][:, nb * 512:(nb + 1) * 512], in_=pb[:, :],
                                                                 func=AF.Identity), reads=[pb], writes=[gate_o[l]])

        S.dma("sp", gate_d.ap()[l, 0], gate_p[l][:], reads=[gate_p[l]], writes=[gate_d])
        S.dma("sp", gate_d.ap()[l, 1], gate_o[l][:], reads=[gate_o[l]], writes=[gate_d])

    for half in range(2):
        wt = wst[wi % 2]
        wi += 1
        S.dma("sp", wt[:, :, 0:320], wA.ap()[:, half * 320:(half + 1) * 320].rearrange("(kt p) n -> p kt n", p=128),
              reads=[wA], writes=[wt])
        S.op("pool", lambda e, wt=wt, half=half: e.tensor_copy(out=wA_sb[:, :, half * 320:(half + 1) * 320],
                                                               in_=wt[:, :, 0:320]),
             reads=[wt], writes=[wA_sb])
    S.dma("sp", qkg_sb[:], qkg.ap(), reads=[qkg], writes=[qkg_sb])

    P0.close()
    P1 = S.phase()
    PI = math.pi
    S.dma("sp", d_sb[:], dpp.ap(), reads=[dpp], writes=[d_sb])

    def trig(P, name, theta, shape):
        ki = P.sb(name + "_ki", shape, I32)
        kf = P.sb(name + "_kf", shape, F32)
        r = P.sb(name + "_r", shape, F32)
        r2 = P.sb(name + "_r2", shape, F32)
        c = P.sb(name + "_c", shape, F32)
        sn = P.sb(name + "_s", shape, F32)
        tmp = P.sb(name + "_t", shape, F32)
        S.op("dve", lambda e: e.tensor_scalar(out=kf[:], in0=theta[:], scalar1=1.0 / (2 * PI), scalar2=None, op0=ALU.mult),
             reads=[theta], writes=[kf])
        S.op("dve", lambda e: e.tensor_copy(out=ki[:], in_=kf[:]), reads=[kf], writes=[ki])
        S.op("dve", lambda e: e.tensor_copy(out=kf[:], in_=ki[:]), reads=[ki], writes=[kf])
        S.op("dve", lambda e: e.scalar_tensor_tensor(out=r[:], in0=kf[:], scalar=-2 * PI, in1=theta[:], op0=ALU.mult, op1=ALU.add),
             reads=[kf, theta], writes=[r])

        def wrap(t):
            S.op("dve", lambda e: e.tensor_scalar(out=tmp[:], in0=t[:], scalar1=PI, scalar2=-2 * PI, op0=ALU.is_gt, op1=ALU.mult),
                 reads=[t], writes=[tmp])
            S.op("dve", lambda e: e.tensor_tensor(out=t[:], in0=t[:], in1=tmp[:], op=ALU.add), reads=[t, tmp], writes=[t])
            S.op("dve", lambda e: e.tensor_scalar(out=tmp[:], in0=t[:], scalar1=-PI, scalar2=2 * PI, op0=ALU.is_lt, op1=ALU.mult),
                 reads=[t], writes=[tmp])
            S.op("dve", lambda e: e.tensor_tensor(out=t[:], in0=t[:], in1=tmp[:], op=ALU.add), reads=[t, tmp], writes=[t])

        wrap(r)
        S.op("dve", lambda e: e.tensor_scalar(out=r2[:], in0=r[:], scalar1=PI / 2, scalar2=None, op0=ALU.add), reads=[r], writes=[r2])
        wrap(r2)
        S.op("act", lambda e: e.activation(out=sn[:], in_=r[:], func=AF.Sin), reads=[r], writes=[sn])
        S.op("act", lambda e: e.activation(out=c[:], in_=r2[:], func=AF.Sin), reads=[r2], writes=[c])
        return c, sn

    pp = P1.sb("pp", [128, 4, 3], F32)
    S.dma("sp", pp[:], s5pp.ap(), reads=[s5pp], writes=[pp])
    dt_pp = P1.sb("dt_pp", [128, 4], F32)
    th_pp = P1.sb("th_pp", [128, 4], F32)
    r_pp = P1.sb("r_pp", [128, 4], F32)
    S.op("act", lambda e: e.activation(out=dt_pp[:], in_=pp[:, :, 2], func=AF.Exp), reads=[pp], writes=[dt_pp])
    S.op("dve", lambda e: e.tensor_tensor(out=th_pp[:], in0=pp[:, :, 1], in1=dt_pp[:], op=ALU.mult), reads=[pp, dt_pp], writes=[th_pp])
    S.op("dve", lambda e: e.tensor_tensor(out=r_pp[:], in0=pp[:, :, 0], in1=dt_pp[:], op=ALU.mult), reads=[pp, dt_pp], writes=[r_pp])
    S.op("act", lambda e: e.activation(out=r_pp[:], in_=r_pp[:], func=AF.Exp), reads=[r_pp], writes=[r_pp])
    c1, s1 = trig(P1, "tpp", th_pp, [128, 4])
    tmpc = P1.sb("tmpc", [128, 256], F32)
    for ct in range(4):
        S.op("pool", lambda e, ct=ct: e.memset(cosT[:, ct, 0:1], 1.0), writes=[cosT])
        S.op("pool", lambda e, ct=ct: e.memset(sinT[:, ct, 0:1], 0.0), writes=[sinT])
        S.op("dve", lambda e, ct=ct: e.tensor_copy(out=cosT[:, ct, 1:2], in_=c1[:, ct:ct + 1]), reads=[c1, cosT], writes=[cosT])
        S.op("dve", lambda e, ct=ct: e.tensor_copy(out=sinT[:, ct, 1:2], in_=s1[:, ct:ct + 1]), reads=[s1, sinT], writes=[sinT])
        n = 2
        while n < 512:
            h = n // 2
            S.op("dve", lambda e, ct=ct, n=n, h=h: e.tensor_tensor(out=tmpc[:, 0:1], in0=sinT[:, ct, h:h + 1], in1=sinT[:, ct, h:h + 1], op=ALU.mult),
                 reads=[sinT], writes=[tmpc])
            S.op("dve", lambda e, ct=ct, n=n, h=h: e.scalar_tensor_tensor(out=cosT[:, ct, n:n + 1], in0=cosT[:, ct, h:h + 1], scalar=cosT[:, ct, h:h + 1],
                                                                    in1=tmpc[:, 0:1], op0=ALU.mult, op1=ALU.subtract),
                 reads=[cosT, tmpc], writes=[cosT])
            S.op("dve", lambda e, ct=ct, n=n, h=h: e.tensor_tensor(out=tmpc[:, 1:2], in0=sinT[:, ct, h:h + 1], in1=cosT[:, ct, h:h + 1], op=ALU.mult),
                 reads=[sinT, cosT], writes=[tmpc])
            S.op("dve", lambda e, ct=ct, n=n: e.tensor_scalar(out=sinT[:, ct, n:n + 1], in0=tmpc[:, 1:2], scalar1=2.0, scalar2=None, op0=ALU.mult),
                 reads=[tmpc, sinT], writes=[sinT])
            m = n - 1
            if m > 0:
                S.op("dve", lambda e, ct=ct, n=n, m=m: e.tensor_scalar(out=tmpc[:, 0:m], in0=sinT[:, ct, 1:n], scalar1=sinT[:, ct, n:n + 1], scalar2=None, op0=ALU.mult),
                     reads=[sinT], writes=[tmpc])
                S.op("dve", lambda e, ct=ct, n=n, m=m: e.scalar_tensor_tensor(out=cosT[:, ct, n + 1:2 * n], in0=cosT[:, ct, 1:n], scalar=cosT[:, ct, n:n + 1],
                                                                        in1=tmpc[:, 0:m], op0=ALU.mult, op1=ALU.subtract),
                     reads=[cosT, tmpc], writes=[cosT])
                S.op("dve", lambda e, ct=ct, n=n, m=m: e.tensor_scalar(out=tmpc[:, 0:m], in0=cosT[:, ct, 1:n], scalar1=sinT[:, ct, n:n + 1], scalar2=None, op0=ALU.mult),
                     reads=[cosT, sinT], writes=[tmpc])
                S.op("dve", lambda e, ct=ct, n=n, m=m: e.scalar_tensor_tensor(out=sinT[:, ct, n + 1:2 * n], in0=sinT[:, ct, 1:n], scalar=cosT[:, ct, n:n + 1],
                                                                        in1=tmpc[:, 0:m], op0=ALU.mult, op1=ALU.add),
                     reads=[sinT, cosT, tmpc], writes=[sinT])
            n *= 2
        S.op("dve", lambda e, ct=ct: e.tensor_tensor(out=tmpc[:, 0:1], in0=sinT[:, ct, 511:512], in1=sinT[:, ct, 1:2], op=ALU.mult),
             reads=[sinT], writes=[tmpc])
        S.op("dve", lambda e, ct=ct: e.scalar_tensor_tensor(out=rotT[:, ct, 0:1], in0=cosT[:, ct, 511:512], scalar=cosT[:, ct, 1:2], in1=tmpc[:, 0:1],
                                                      op0=ALU.mult, op1=ALU.subtract), reads=[cosT, tmpc], writes=[rotT])
        S.op("dve", lambda e, ct=ct: e.tensor_tensor(out=tmpc[:, 1:2], in0=sinT[:, ct, 511:512], in1=cosT[:, ct, 1:2], op=ALU.mult),
             reads=[sinT, cosT], writes=[tmpc])
        S.op("dve", lambda e, ct=ct: e.scalar_tensor_tensor(out=rotT[:, ct, 1:2], in0=cosT[:, ct, 511:512], scalar=sinT[:, ct, 1:2], in1=tmpc[:, 1:2],
                                                      op0=ALU.mult, op1=ALU.add), reads=[cosT, sinT, tmpc], writes=[rotT])
        S.op("dve", lambda e, ct=ct: e.tensor_tensor(out=rot1r[:, ct, 0:1], in0=c1[:, ct:ct + 1], in1=r_pp[:, ct:ct + 1], op=ALU.mult),
             reads=[c1, r_pp], writes=[rot1r])
        S.op("dve", lambda e, ct=ct: e.tensor_tensor(out=rot1r[:, ct, 1:2], in0=s1[:, ct:ct + 1], in1=r_pp[:, ct:ct + 1], op=ALU.mult),
             reads=[s1, r_pp, rot1r], writes=[rot1r])
        S.op("pool", lambda e, ct=ct: e.memset(rdec[:, ct, :], 1.0), writes=[rdec])
        S.op("dve", lambda e, ct=ct: e.tensor_scalar(out=rdec[:, ct, :], in0=rdec[:, ct, :], scalar1=r_pp[:, ct:ct + 1], scalar2=None, op0=ALU.mult),
             reads=[rdec, r_pp], writes=[rdec])
        S.op("dve", lambda e, ct=ct: e.tensor_copy(out=rdec_s[:, ct, :], in_=rdec[:, ct, :]), reads=[rdec], writes=[rdec_s])
        S.op("dve", lambda e, ct=ct: e.memset(rdec_s[:, ct, 0:512:32], 0.0), reads=[rdec_s], writes=[rdec_s])

    bc = P1.sb("bc", [128, 3, 512], F32)
    S.dma("sp", bc[:], s5bc.ap().rearrange("(o r) n -> o r n", o=1).to_broadcast([128, 3, 512]), reads=[s5bc], writes=[bc])
    dt_bc = P1.sb("dt_bc", [128, 512], F32)
    th_bc = P1.sb("th_bc", [128, 512], F32)
    mg_bc = P1.sb("mg_bc", [128, 512], F32)
    S.op("act", lambda e: e.activation(out=dt_bc[:], in_=bc[:, 2, :], func=AF.Exp), reads=[bc], writes=[dt_bc])
    S.op("dve", lambda e: e.tensor_tensor(out=th_bc[:], in0=bc[:, 1, :], in1=dt_bc[:], op=ALU.mult), reads=[bc, dt_bc], writes=[th_bc])
    S.op("dve", lambda e: e.tensor_tensor(out=mg_bc[:], in0=bc[:, 0, :], in1=dt_bc[:], op=ALU.mult), reads=[bc, dt_bc], writes=[mg_bc])
    S.op("act", lambda e: e.activation(out=mg_bc[:], in_=mg_bc[:], func=AF.Exp), reads=[mg_bc], writes=[mg_bc])
    cb, sb_ = trig(P1, "tbc", th_bc, [128, 512])
    ar1 = P1.sb("ar1", [128, 512], F32)
    ai_ = P1.sb("ai_", [128, 512], F32)
    den = P1.sb("den", [128, 512], F32)
    zr = P1.sb("zr", [128, 512], F32)
    zi = P1.sb("zi", [128, 512], F32)
    t1 = P1.sb("t1", [128, 512], F32)
    S.op("dve", lambda e: e.tensor_tensor(out=ar1[:], in0=mg_bc[:], in1=cb[:], op=ALU.mult), reads=[mg_bc, cb], writes=[ar1])
    S.op("dve", lambda e: e.tensor_scalar(out=ar1[:], in0=ar1[:], scalar1=-1.0, scalar2=None, op0=ALU.add), reads=[ar1], writes=[ar1])
    S.op("dve", lambda e: e.tensor_tensor(out=ai_[:], in0=mg_bc[:], in1=sb_[:], op=ALU.mult), reads=[mg_bc, sb_], writes=[ai_])
    S.op("dve", lambda e: e.tensor_tensor(out=den[:], in0=bc[:, 0, :], in1=bc[:, 0, :], op=ALU.mult), reads=[bc], writes=[den])
    S.op("dve", lambda e: e.tensor_tensor(out=t1[:], in0=bc[:, 1, :], in1=bc[:, 1, :], op=ALU.mult), reads=[bc], writes=[t1])
    S.op("dve", lambda e: e.tensor_tensor(out=den[:], in0=den[:], in1=t1[:], op=ALU.add), reads=[den, t1], writes=[den])
    S.op("dve", lambda e: e.reciprocal(out=den[:], in_=den[:]), reads=[den], writes=[den])
    S.op("dve", lambda e: e.tensor_tensor(out=zr[:], in0=ar1[:], in1=bc[:, 0, :], op=ALU.mult), reads=[ar1, bc], writes=[zr])
    S.op("dve", lambda e: e.tensor_tensor(out=t1[:], in0=ai_[:], in1=bc[:, 1, :], op=ALU.mult), reads=[ai_, bc], writes=[t1])
    S.op("dve", lambda e: e.tensor_tensor(out=zr[:], in0=zr[:], in1=t1[:], op=ALU.add), reads=[zr, t1], writes=[zr])
    S.op("dve", lambda e: e.tensor_tensor(out=zr[:], in0=zr[:], in1=den[:], op=ALU.mult), reads=[zr, den], writes=[zr])
    S.op("dve", lambda e: e.tensor_tensor(out=zi[:], in0=ai_[:], in1=bc[:, 0, :], op=ALU.mult), reads=[ai_, bc], writes=[zi])
    S.op("dve", lambda e: e.tensor_tensor(out=t1[:], in0=ar1[:], in1=bc[:, 1, :], op=ALU.mult), reads=[ar1, bc], writes=[t1])
    S.op("dve", lambda e: e.tensor_tensor(out=zi[:], in0=zi[:], in1=t1[:], op=ALU.subtract), reads=[zi, t1], writes=[zi])
    S.op("dve", lambda e: e.tensor_tensor(out=zi[:], in0=zi[:], in1=den[:], op=ALU.mult), reads=[zi, den], writes=[zi])
    brf = P1.sb("brf", [128, 512], F32)
    bif = P1.sb("bif", [128, 512], F32)
    S.dma("sp", brf[:], brT.ap().rearrange("p a c -> p (a c)"), reads=[brT], writes=[brf])
    S.dma("sp", bif[:], biT.ap().rearrange("p a c -> p (a c)"), reads=[biT], writes=[bif])
    t2 = P1.sb("t2", [128, 512], F32)
    S.op("dve", lambda e: e.tensor_tensor(out=t1[:], in0=zr[:], in1=brf[:], op=ALU.mult), reads=[zr, brf], writes=[t1])
    S.op("dve", lambda e: e.tensor_tensor(out=t2[:], in0=zi[:], in1=bif[:], op=ALU.mult), reads=[zi, bif], writes=[t2])
    S.op("dve", lambda e: e.tensor_tensor(out=BreT[:].rearrange("p a c -> p (a c)"), in0=t1[:], in1=t2[:], op=ALU.subtract),
         reads=[t1, t2], writes=[BreT])
    S.op("dve", lambda e: e.tensor_tensor(out=t1[:], in0=zr[:], in1=bif[:], op=ALU.mult), reads=[zr, bif], writes=[t1])
    S.op("dve", lambda e: e.tensor_tensor(out=t2[:], in0=zi[:], in1=brf[:], op=ALU.mult), reads=[zi, brf], writes=[t2])
    S.op("dve", lambda e: e.tensor_tensor(out=BimT[:].rearrange("p a c -> p (a c)"), in0=t1[:], in1=t2[:], op=ALU.add),
         reads=[t1, t2], writes=[BimT])
    S.dma("sp", brf[:], crT.ap().rearrange("p a c -> p (a c)"), reads=[crT], writes=[brf])
    S.dma("sp", bif[:], ciT.ap().rearrange("p a c -> p (a c)"), reads=[ciT], writes=[bif])
    S.op("dve", lambda e: e.tensor_copy(out=CreT[:].rearrange("p a c -> p (a c)"), in_=brf[:]), reads=[brf], writes=[CreT])
    S.op("dve", lambda e: e.tensor_copy(out=CimT[:].rearrange("p a c -> p (a c)"), in_=bif[:]), reads=[bif], writes=[CimT])
    stf = [P1.sb("stf%d" % i, [32, 512], F32) for i in range(2)]
    S.dma("sp", stf[0][:], st_re.ap(), reads=[st_re], writes=[stf[0]])
    S.dma("sp", stf[1][:], st_im.ap(), reads=[st_im], writes=[stf[1]])
    hin = P1.sb("hin", [128, 4, 2, 32], F32)
    for ct in range(4):
        for ri in range(2):
            pb = gen_bank()
            S.op("pe", lambda e, pb=pb, ct=ct, ri=ri: e.transpose(pb[:, 0:32], stf[ri][0:32, ct * 128:(ct + 1) * 128], ident_f[0:32, 0:32]),
                 reads=[stf[ri], ident_f], writes=[pb])
            S.op("dve", lambda e, pb=pb, ct=ct, ri=ri: e.tensor_copy(out=hin[:, ct, ri, :], in_=pb[:, 0:32]), reads=[pb], writes=[hin])
        S.op("dve", lambda e, ct=ct: e.tensor_scalar(out=tmpc[:, 0:32], in0=hin[:, ct, 1, :], scalar1=rot1r[:, ct, 1:2], scalar2=None, op0=ALU.mult),
             reads=[hin, rot1r], writes=[tmpc])
        S.op("dve", lambda e, ct=ct: e.scalar_tensor_tensor(out=ginit_s[:, ct, 0, :], in0=hin[:, ct, 0, :], scalar=rot1r[:, ct, 0:1], in1=tmpc[:, 0:32],
                                                      op0=ALU.mult, op1=ALU.subtract), reads=[hin, rot1r, tmpc], writes=[ginit_s])
        S.op("dve", lambda e, ct=ct: e.tensor_scalar(out=tmpc[:, 32:64], in0=hin[:, ct, 0, :], scalar1=rot1r[:, ct, 1:2], scalar2=None, op0=ALU.mult),
             reads=[hin, rot1r], writes=[tmpc])
        S.op("dve", lambda e, ct=ct: e.scalar_tensor_tensor(out=ginit_s[:, ct, 1, :], in0=hin[:, ct, 1, :], scalar=rot1r[:, ct, 0:1], in1=tmpc[:, 32:64],
                                                      op0=ALU.mult, op1=ALU.add), reads=[hin, rot1r, tmpc, ginit_s], writes=[ginit_s])
    for ct in range(4):
        for hf in range(2):
            sl = slice(hf * 16, hf * 16 + 16)
            c_ = cosT[:, ct, 0:512:32]
            s_ = sinT[:, ct, 0:512:32]
            S.op("dve", lambda e, ct=ct, sl=sl, c_=c_: e.tensor_tensor(out=tmpc[:, 0:16], in0=ginit_s[:, ct, 0, sl], in1=c_, op=ALU.mult), reads=[ginit_s, cosT], writes=[tmpc])
            S.op("dve", lambda e, ct=ct, sl=sl, s_=s_: e.tensor_tensor(out=tmpc[:, 16:32], in0=ginit_s[:, ct, 1, sl], in1=s_, op=ALU.mult), reads=[ginit_s, sinT, tmpc], writes=[tmpc])
            S.op("dve", lambda e, ct=ct, sl=sl, c_=c_: e.tensor_tensor(out=tmpc[:, 32:48], in0=ginit_s[:, ct, 1, sl], in1=c_, op=ALU.mult), reads=[ginit_s, cosT, tmpc], writes=[tmpc])
            S.op("dve", lambda e, ct=ct, sl=sl, s_=s_: e.tensor_tensor(out=tmpc[:, 48:64], in0=ginit_s[:, ct, 0, sl], in1=s_, op=ALU.mult), reads=[ginit_s, sinT, tmpc], writes=[tmpc])
            S.op("dve", lambda e, ct=ct, sl=sl: e.tensor_tensor(out=ginit_s[:, ct, 0, sl], in0=tmpc[:, 0:16], in1=tmpc[:, 16:32], op=ALU.add), reads=[tmpc, ginit_s], writes=[ginit_s])
            S.op("dve", lambda e, ct=ct, sl=sl: e.tensor_tensor(out=ginit_s[:, ct, 1, sl], in0=tmpc[:, 32:48], in1=tmpc[:, 48:64], op=ALU.subtract), reads=[tmpc, ginit_s], writes=[ginit_s])
    lam_sb = P1.sb("lam_sb", [128, 4, 64], F32)
    S.dma("sp", lam_sb[:], lamv.ap().rearrange("(o r) n -> o r n", o=1).to_broadcast([128, 4, 64]), reads=[lamv], writes=[lam_sb])
    lp = P1.sb("lp", [128, 2, 64], F32)
    le = P1.sb("le", [128, 2], F32)
    S.op("dve", lambda e: e.tensor_tensor(out=lp[:, 0, :], in0=lam_sb[:, 0, :], in1=lam_sb[:, 1, :], op=ALU.mult), reads=[lam_sb], writes=[lp])
    S.op("dve", lambda e: e.tensor_tensor(out=lp[:, 1, :], in0=lam_sb[:, 2, :], in1=lam_sb[:, 3, :], op=ALU.mult), reads=[lam_sb, lp], writes=[lp])
    S.op("dve", lambda e: e.tensor_reduce(out=le[:], in_=lp[:], axis=AX.X, op=ALU.add), reads=[lp], writes=[le])
    S.op("act", lambda e: e.activation(out=le[:], in_=le[:], func=AF.Exp), reads=[le], writes=[le])
    S.op("dve", lambda e: e.tensor_tensor(out=neg_lam[:], in0=le[:, 1:2], in1=le[:, 0:1], op=ALU.subtract), reads=[le], writes=[neg_lam])
    S.op("dve", lambda e: e.tensor_scalar(out=neg_lam[:], in0=neg_lam[:], scalar1=-LAM_INIT0, scalar2=None, op0=ALU.add), reads=[neg_lam], writes=[neg_lam])
    S.dma("sp", gsc[:], aog.ap(), reads=[aog], writes=[gsc])
    S.op("dve", lambda e: e.tensor_scalar(out=gsc[:], in0=gsc[:], scalar1=1.0 - LAM_INIT0, scalar2=None, op0=ALU.mult), reads=[gsc], writes=[gsc])
    if DEBUG:
        S.dma("sp", dbg1.ap()[0], cosT[:], reads=[cosT], writes=[dbg1])
        S.dma("sp", dbg1.ap()[1], sinT[:], reads=[sinT], writes=[dbg1])
        S.dma("sp", dbg1.ap()[2], rdec_s[:], reads=[rdec_s], writes=[dbg1])
        S.dma("sp", dbg2.ap()[0], BreT[:], reads=[BreT], writes=[dbg2])
        S.dma("sp", dbg2.ap()[1], BimT[:], reads=[BimT], writes=[dbg2])
        for sq_ in range(32):
            S.dma("sp", dbg3.ap()[sq_], ginit_s[:, :, :, sq_], reads=[ginit_s], writes=[dbg3],
                  fn=lambda e, sq_=sq_: e.dma_start(out=dbg3.ap()[sq_], in_=ginit_s[:, :, :, sq_], allow_slow_non_contiguous=True))
        S.dma("sp", dbg3.ap()[32], rotT[:], reads=[rotT], writes=[dbg3])
        S.dma("sp", dbg3.ap()[33], rot1r[:], reads=[rot1r], writes=[dbg3])
    P1.close()

    PA = S.phase() if os.environ.get("KPA", "1") == "1" else S
    xt = [PA.sb("xt%d" % i, [128, D], F32) for i in range(2)]
    xn = [PA.sb("xn0", [128, D], BF16)] * 2
    junk = xn[0]
    ss = [PA.sb("ss%d" % i, [128, 1], F32) for i in range(4)]
    rs = [PA.sb("rs%d" % i, [128, 1], F32) for i in range(4)]
    hT = [PA.sb("hT0", [128, 16, 512], BF16)] * 2
    qa = [[PA.sb("qa%d_%d" % (m, i), [68, 512], BF16) for i in range(2)] for m in range(2)]
    ka = [[PA.sb("ka%d_%d" % (m, i), [68, 512], BF16) for i in range(2)] for m in range(2)]
    knf = [PA.sb("knf%d" % m, [64, 512], F32) for m in range(2)]
    sq = [PA.sb("sq0", [64, 512], F32)] * 2
    rstd_t = [PA.sb("rstd_t0", [64, 512], F32)] * 2
    uT = [PA.sb("uT0", [128, 512], BF16)] * 2
    zaT = [PA.sb("zaT0", [128, 512], F32)] * 2
    vtok_f = [PA.sb("vtok_f0", [128, 4, 128], F32)] * 2
    vtok_b = [PA.sb("vtok_b%d" % i, [128, 4, 128], BF16) for i in range(2)]
    ktok_f = [PA.sb("ktok_f0", [128, 4, 128], F32)] * 2

    uTf = [PA.sb("uTf0", [128, 512], F32)] * 2
    Xr = PA.sb("Xr", [128, 512], F32)
    Xi = PA.sb("Xi", [128, 512], F32)
    ta = [PA.sb("ta%d" % i, [128, 512], F32) for i in range(2)]
    tb = [PA.sb("tb%d" % i, [128, 512], F32) for i in range(2)]
    gr = PA.sb("gr", [128, 512], F32)
    gi = PA.sb("gi", [128, 512], F32)
    hrb = [PA.sb("hrb%d" % i, [128, 512], BF16) for i in range(4)]
    hib = [PA.sb("hib%d" % i, [128, 512], BF16) for i in range(4)]
    gcar = PA.sb("gcar", [128, 4, 2], F32)
    gin = PA.sb("gin", [128, 4, 2], F32)
    stP = PA.sb("stP", [128, 2, 4], F32)
    stP_t = PA.sb("stP_t", [8, 128], F32)
    stS = PA.sb("stS", [128, 4, 2, 32], F32)
    yTf = ta[0]
    gw = [Xr, Xi]
    gyT = [PA.sb("gyT%d" % i, [128, 512], BF16) for i in range(2)]
    tile_ctr = [0]

    def s5_front(b):
        p = b % 2
        sample = b >= NPB
        for ct in range(4):
            pr = gen_bank()
            S.op("pe", lambda e, pr=pr, ct=ct: e.matmul(pr[:, :], lhsT=BreT[:, ct, :], rhs=uT[p][:], start=True, stop=True),
                 reads=[BreT, uT[p]], writes=[pr])
            pi_ = gen_bank()
            S.op("pe", lambda e, pi_=pi_, ct=ct: e.matmul(pi_[:, :], lhsT=BimT[:, ct, :], rhs=uT[p][:], start=True, stop=True),
                 reads=[BimT, uT[p]], writes=[pi_])
            cT_ = cosT[:, ct, :]
            sT_ = sinT[:, ct, :]
            ta0, tb0 = ta[0], tb[0]
            S.op("dve", lambda e, pr=pr, cT_=cT_: e.tensor_tensor(out=ta0[:], in0=pr[:, :], in1=cT_, op=ALU.mult), reads=[pr, cosT], writes=[ta0])
            S.op("dve", lambda e, pi_=pi_, sT_=sT_: e.tensor_tensor(out=tb0[:], in0=pi_[:, :], in1=sT_, op=ALU.mult), reads=[pi_, sinT], writes=[tb0])
            S.op("dve", lambda e: e.tensor_tensor(out=Xr[:], in0=ta0[:], in1=tb0[:], op=ALU.add), reads=[ta0, tb0], writes=[Xr])
            S.op("dve", lambda e, pi_=pi_, cT_=cT_: e.tensor_tensor(out=ta0[:], in0=pi_[:, :], in1=cT_, op=ALU.mult), reads=[pi_, cosT], writes=[ta0])
            S.op("dve", lambda e, pr=pr, sT_=sT_: e.tensor_tensor(out=tb0[:], in0=pr[:, :], in1=sT_, op=ALU.mult), reads=[pr, sinT], writes=[tb0])
            S.op("dve", lambda e: e.tensor_tensor(out=Xi[:], in0=ta0[:], in1=tb0[:], op=ALU.subtract), reads=[ta0, tb0], writes=[Xi])
            if not sample:
                if b == 0:
                    ir = ii = 0.0
                    rd = []
                else:
                    S.op("dve", lambda e, ct=ct: e.tensor_scalar(out=gin[:, ct, 0:1], in0=gcar[:, ct, 1:2], scalar1=rotT[:, ct, 1:2], scalar2=None, op0=ALU.mult),
                         reads=[gcar, rotT], writes=[gin])
                    S.op("dve", lambda e, ct=ct: e.scalar_tensor_tensor(out=gin[:, ct, 0:1], in0=gcar[:, ct, 0:1], scalar=rotT[:, ct, 0:1], in1=gin[:, ct, 0:1],
                                                                  op0=ALU.mult, op1=ALU.subtract), reads=[gcar, rotT, gin], writes=[gin])
                    S.op("dve", lambda e, ct=ct: e.tensor_scalar(out=gin[:, ct, 1:2], in0=gcar[:, ct, 0:1], scalar1=rotT[:, ct, 1:2], scalar2=None, op0=ALU.mult),
                         reads=[gcar, rotT, gin], writes=[gin])
                    S.op("dve", lambda e, ct=ct: e.scalar_tensor_tensor(out=gin[:, ct, 1:2], in0=gcar[:, ct, 1:2], scalar=rotT[:, ct, 0:1], in1=gin[:, ct, 1:2],
                                                                  op0=ALU.mult, op1=ALU.add), reads=[gcar, rotT, gin], writes=[gin])
                    ir, ii = gin[:, ct, 0:1], gin[:, ct, 1:2]
                    rd = [gin]
                dec = rdec
            else:
                s0 = (b - NPB) * 16
                S.op("dve", lambda e, ct=ct, s0=s0: e.tensor_tensor(out=Xr[:, 0:512:32], in0=Xr[:, 0:512:32], in1=ginit_s[:, ct, 0, s0:s0 + 16], op=ALU.add),
                     reads=[Xr, ginit_s], writes=[Xr])
                S.op("dve", lambda e, ct=ct, s0=s0: e.tensor_tensor(out=Xi[:, 0:512:32], in0=Xi[:, 0:512:32], in1=ginit_s[:, ct, 1, s0:s0 + 16], op=ALU.add),
                     reads=[Xi, ginit_s], writes=[Xi])
                ir = ii = 0.0
                rd = []
                dec = rdec_s
            S.op("dve", lambda e, ct=ct, ir=ir, dec=dec: e.tensor_tensor_scan(out=gr[:], data0=dec[:, ct, :], data1=Xr[:], initial=ir, op0=ALU.mult, op1=ALU.add),
                 reads=[dec, Xr] + rd, writes=[gr])
            S.op("dve", lambda e, ct=ct, ii=ii, dec=dec: e.tensor_tensor_scan(out=gi[:], data0=dec[:, ct, :], data1=Xi[:], initial=ii, op0=ALU.mult, op1=ALU.add),
                 reads=[dec, Xi] + rd, writes=[gi])
            if not sample:
                S.op("dve", lambda e, ct=ct: e.tensor_copy(out=gcar[:, ct, 0:1], in_=gr[:, 511:512]), reads=[gr], writes=[gcar])
                S.op("dve", lambda e, ct=ct: e.tensor_copy(out=gcar[:, ct, 1:2], in_=gi[:, 511:512]), reads=[gi, gcar], writes=[gcar])
                if b == NPB - 1:
                    S.op("dve", lambda e, ct=ct: e.tensor_scalar(out=stP[:, 0, ct:ct + 1], in0=gi[:, 511:512], scalar1=sinT[:, ct, 511:512], scalar2=None, op0=ALU.mult),
                         reads=[gi, sinT], writes=[stP])
                    S.op("dve", lambda e, ct=ct: e.scalar_tensor_tensor(out=stP[:, 0, ct:ct + 1], in0=gr[:, 511:512], scalar=cosT[:, ct, 511:512], in1=stP[:, 0, ct:ct + 1],
                                                                  op0=ALU.mult, op1=ALU.subtract), reads=[gr, cosT, stP], writes=[stP])
                    S.op("dve", lambda e, ct=ct: e.tensor_scalar(out=stP[:, 1, ct:ct + 1], in0=gr[:, 511:512], scalar1=sinT[:, ct, 511:512], scalar2=None, op0=ALU.mult),
                         reads=[gr, sinT, stP], writes=[stP])
                    S.op("dve", lambda e, ct=ct: e.scalar_tensor_tensor(out=stP[:, 1, ct:ct + 1], in0=gi[:, 511:512], scalar=cosT[:, ct, 511:512], in1=stP[:, 1, ct:ct + 1],
                                                                  op0=ALU.mult, op1=ALU.add), reads=[gi, cosT, stP], writes=[stP])
            else:
                s0 = (b - NPB) * 16
                gre, gie = gr[:, 31:512:32], gi[:, 31:512:32]
                ce, se = cosT[:, ct, 31:512:32], sinT[:, ct, 31:512:32]
                S.op("dve", lambda e, gre=gre, ce=ce: e.tensor_tensor(out=ta0[:, 0:16], in0=gre, in1=ce, op=ALU.mult), reads=[gr, cosT], writes=[ta0])
                S.op("dve", lambda e, gie=gie, se=se: e.tensor_tensor(out=tb0[:, 0:16], in0=gie, in1=se, op=ALU.mult), reads=[gi, sinT], writes=[tb0])
                S.op("dve", lambda e, ct=ct, s0=s0: e.tensor_tensor(out=stS[:, ct, 0, s0:s0 + 16], in0=ta0[:, 0:16], in1=tb0[:, 0:16], op=ALU.subtract),
                     reads=[ta0, tb0], writes=[stS])
                S.op("dve", lambda e, gre=gre, se=se: e.tensor_tensor(out=ta0[:, 0:16], in0=gre, in1=se, op=ALU.mult), reads=[gr, sinT], writes=[ta0])
                S.op("dve", lambda e, gie=gie, ce=ce: e.tensor_tensor(out=tb0[:, 0:16], in0=gie, in1=ce, op=ALU.mult), reads=[gi, cosT], writes=[tb0])
                S.op("dve", lambda e, ct=ct, s0=s0: e.tensor_tensor(out=stS[:, ct, 1, s0:s0 + 16], in0=ta0[:, 0:16], in1=tb0[:, 0:16], op=ALU.add),
                     reads=[ta0, tb0, stS], writes=[stS])
            ta1, tb1 = ta[1], tb[1]
            S.op("dve", lambda e, cT_=cT_: e.tensor_tensor(out=ta1[:], in0=gr[:], in1=cT_, op=ALU.mult), reads=[gr, cosT], writes=[ta1])
            S.op("dve", lambda e, sT_=sT_: e.tensor_tensor(out=tb1[:], in0=gi[:], in1=sT_, op=ALU.mult), reads=[gi, sinT], writes=[tb1])
            S.op("dve", lambda e, ct=ct: e.tensor_tensor(out=hrb[ct][:], in0=ta1[:], in1=tb1[:], op=ALU.subtract), reads=[ta1, tb1], writes=[hrb[ct]])
            S.op("dve", lambda e, sT_=sT_: e.tensor_tensor(out=ta0[:], in0=gr[:], in1=sT_, op=ALU.mult), reads=[gr, sinT], writes=[ta0])
            S.op("dve", lambda e, cT_=cT_: e.tensor_tensor(out=tb0[:], in0=gi[:], in1=cT_, op=ALU.mult), reads=[gi, cosT], writes=[tb0])
            S.op("dve", lambda e, ct=ct: e.scalar_tensor_tensor(out=hib[ct][:], in0=ta0[:], scalar=-1.0, in1=tb0[:], op0=ALU.mult, op1=ALU.subtract),
                 reads=[ta0, tb0], writes=[hib[ct]])

    def s5_back(b):
        p = b % 2
        py = gen_bank()
        for ct in range(4):
            S.op("pe", lambda e, py=py, ct=ct: e.matmul(py[:, :], lhsT=CreT[:, ct, :], rhs=hrb[ct][:], start=(ct == 0), stop=False),
                 reads=[CreT, hrb[ct]], writes=[py])
            S.op("pe", lambda e, py=py, ct=ct: e.matmul(py[:, :], lhsT=CimT[:, ct, :], rhs=hib[ct][:], start=False, stop=(ct == 3)),
                 reads=[CimT, hib[ct]], writes=[py])
        S.op("dve", lambda e, py=py: e.scalar_tensor_tensor(out=yTf[:], in0=uTf[p][:], scalar=d_sb[:, 0:1], in1=py[:, :], op0=ALU.mult, op1=ALU.add),
             reads=[uTf[p], d_sb, py], writes=[yTf])
        g0, g1 = gw
        S.op("dve", lambda e: e.tensor_tensor(out=g0[:], in0=yTf[:], in1=yTf[:], op=ALU.mult), reads=[yTf], writes=[g0])
        S.op("dve", lambda e: e.tensor_scalar(out=g0[:], in0=g0[:], scalar1=0.044715, scalar2=1.0, op0=ALU.mult, op1=ALU.add), reads=[g0], writes=[g0])
        S.op("dve", lambda e: e.tensor_tensor(out=g0[:], in0=g0[:], in1=yTf[:], op=ALU.mult), reads=[g0, yTf], writes=[g0])
        S.op("act", lambda e: e.activation(out=g1[:], in_=g0[:], func=AF.Exp, scale=-1.5957691216057308), reads=[g0], writes=[g1])
        S.op("dve", lambda e: e.tensor_scalar(out=g1[:], in0=g1[:], scalar1=1.0, scalar2=None, op0=ALU.add), reads=[g1], writes=[g1])
        S.op("dve", lambda e: e.reciprocal(out=g1[:], in_=g1[:]), reads=[g1], writes=[g1])
        S.op("dve", lambda e: e.tensor_tensor(out=gyT[p][:], in0=g1[:], in1=yTf[:], op=ALU.mult), reads=[g1, yTf], writes=[gyT[p]])
        ex_store(gyT[p], 0, b)
        if DEBUG:
            S.dma("pool", dbg_gy.ap()[:, b * 512:(b + 1) * 512], gyT[p][:], reads=[gyT[p]], writes=[dbg_gy])


    kch = [[PA.sb("kch%d_%d" % (m, i), [68, 1024], BF16) for i in range(2)] for m in range(2)]
    vch = [PA.sb("vch%d" % i, [128, 8, 128], BF16) for i in range(2)]
    Pt = [[PA.sb("Pt%d_%d" % (m, i), [128, 512], BF16) for i in range(2)] for m in range(2)]
    biasdiag = PA.sb("biasdiag_sb", [128, 4, 512], F32)
    biasnew = PA.sb("biasnew_sb", [128, 256], F32)
    tmpS = tb[0]
    rl = PA.sb("rl", [128, 512], F32)
    o1n = PA.sb("o1n", [128, 512], F32)
    oT = PA.sb("oT", [128, 512], F32)
    osq = PA.sb("osq", [128, 512], F32)
    attT = [PA.sb("attT%d" % i, [128, 512], BF16) for i in range(2)]
    osb = [Xr, Xi, gr, gi]
    onsb = ta[0]
    kcf = ta[1]
    vcf = tb[1]
    Pn = PA.sb("Pn", [128, 256], BF16)
    S.dma("sp", biasdiag[:], biasdiag_d.ap(), reads=[biasdiag_d], writes=[biasdiag])
    S.dma("sp", biasnew[:], biasnew_d.ap(), reads=[biasnew_d], writes=[biasnew])

    class R_:
        multi = False
        excl = False

        def __init__(self, name):
            self.res = Res(name)
    kvres = [R_("kv%d" % i) for i in range(NPB)]

    def ex_store(src, a, b):
        exv = EX_in.ap().rearrange("(a f r q) c -> a f r (q c)", a=2, f=128, r=8)
        if b < NPB:
            r, off = b // 4, (b % 4) * 512
            S.dma("pool", exv[a, :, r, off:off + 512], src[:], reads=[src], writes=[EX_in])
        else:
            for tt in range(4):
                r = (b - NPB) * 4 + tt
                S.dma("pool", exv[a, :, r, 2048:2176], src[:, tt * 128:(tt + 1) * 128], reads=[src], writes=[EX_in])

    def att_pos_rows(b):
        p = b % 2
        cs = slice(b * 512, (b + 1) * 512)
        for m in range(2):
            S.dma("sp", ka[m][p][64:68, :], kqpos.ap()[0:4, cs], reads=[kqpos], writes=[ka[m][p]])
            S.dma("sp", qa[m][p][64:68, :], kqpos.ap()[4:8, cs], reads=[kqpos], writes=[qa[m][p]])

    def att_store_kv(b):
        p = b % 2
        cs = slice(b * 512, (b + 1) * 512)
        for m in range(2):
            S.dma("pool", kD.ap()[m, :, cs], ka[m][p][:], reads=[ka[m][p]], writes=[kvres[b]])
        S.dma("pool", vD.ap()[cs, :].rearrange("(t p) d -> p t d", p=128), vtok_b[p][:], reads=[vtok_b[p]], writes=[kvres[b]])

    def att_epilogue(b, o1, l1, o2, l2, srcs):
        p = b % 2
        if DEBUG and b == 0:
            for i_, ap_ in enumerate((o1, l1, o2, l2)):
                S.op("dve", lambda e, ap_=ap_: e.tensor_copy(out=osq[:], in_=ap_), reads=srcs, writes=[osq])
                S.dma("sp", dbg_o.ap()[i_], osq[:], reads=[osq], writes=[dbg_o])
            S.dma("sp", dbg_o.ap()[4], zaT[p][:], reads=[zaT[p]], writes=[dbg_o])
        S.op("dve", lambda e: e.reciprocal(out=rl[:], in_=l1), reads=srcs, writes=[rl])
        S.op("dve", lambda e: e.tensor_tensor(out=o1n[:], in0=o1, in1=rl[:], op=ALU.mult), reads=srcs + [rl], writes=[o1n])
        S.op("dve", lambda e: e.reciprocal(out=rl[:], in_=l2), reads=srcs + [rl], writes=[rl])
        S.op("dve", lambda e: e.tensor_tensor(out=oT[:], in0=o2, in1=rl[:], op=ALU.mult), reads=srcs + [rl], writes=[oT])
        S.op("dve", lambda e: e.scalar_tensor_tensor(out=oT[:], in0=oT[:], scalar=neg_lam[:, 0:1], in1=o1n[:], op0=ALU.mult, op1=ALU.add),
             reads=[oT, neg_lam, o1n], writes=[oT])
        S.op("act", lambda e: e.activation(out=osq[:], in_=oT[:], func=AF.Square), reads=[oT], writes=[osq])
        pm = gen_bank()
        S.op("pe", lambda e, pm=pm: e.matmul(pm[:, :], lhsT=ones128f[:], rhs=osq[:], start=True, stop=True), reads=[ones128f, osq], writes=[pm])
        S.op("act", lambda e, pm=pm: e.activation(out=osq[:], in_=pm[:, :], func=AF.Ln, bias=EPS), reads=[pm], writes=[osq])
        S.op("act", lambda e: e.activation(out=osq[:], in_=osq[:], func=AF.Exp, scale=-0.5), reads=[osq], writes=[osq])
        S.op("act", lambda e: e.activation(out=rl[:], in_=zaT[p][:], func=AF.Exp, scale=-1.0), reads=[zaT[p], rl], writes=[rl])
        S.op("dve", lambda e: e.tensor_scalar(out=rl[:], in0=rl[:], scalar1=1.0, scalar2=None, op0=ALU.add), reads=[rl], writes=[rl])
        S.op("dve", lambda e: e.reciprocal(out=rl[:], in_=rl[:]), reads=[rl], writes=[rl])
        S.op("dve", lambda e: e.tensor_tensor(out=rl[:], in0=rl[:], in1=zaT[p][:], op=ALU.mult), reads=[rl, zaT[p]], writes=[rl])
        S.op("dve", lambda e: e.scalar_tensor_tensor(out=oT[:], in0=oT[:], scalar=gsc[:, 0:1], in1=osq[:], op0=ALU.mult, op1=ALU.mult),
             reads=[oT, gsc, osq], writes=[oT])
        S.op("dve", lambda e: e.tensor_tensor(out=attT[p][:], in0=oT[:], in1=rl[:], op=ALU.mult), reads=[oT, rl], writes=[attT[p]])
        ex_store(attT[p], 1, b)
        if DEBUG:
            S.dma("pool", dbg_att.ap()[:, b * 512:(b + 1) * 512], attT[p][:], reads=[attT[p]], writes=[dbg_att])

    def att_prompt(b):
        p = b % 2
        units = []
        nch = (b + 1) // 2
        first_of_chunk = {}

        def load_chunk(c):
            b0 = 2 * c
            nb_ = min(2, b - b0)
            nkt = nb_ * 4
            buf = c % 2
            for m in range(2):
                S.dma("sp", kch[m][buf][:, 0:nkt * 128], kD.ap()[m, :, b0 * 512:b0 * 512 + nkt * 128],
                      reads=[kvres[b0 + i] for i in range(nb_)], writes=[kch[m][buf]])
            S.dma("sp", vch[buf][:, 0:nkt, :], vD.ap()[b0 * 512:b0 * 512 + nkt * 128, :].rearrange("(t p) d -> p t d", p=128),
                  reads=[kvres[b0 + i] for i in range(nb_)], writes=[vch[buf]])

        for c in range(nch):
            b0 = 2 * c
            nkt = min(2, b - b0) * 4
            buf = c % 2
            first_of_chunk[len(units)] = c
            for i in range(nkt):
                units.append(([kch[0][buf][0:68, i * 128:(i + 1) * 128], kch[1][buf][0:68, i * 128:(i + 1) * 128]],
                              vch[buf][:, i, :], None, [kch[0][buf], kch[1][buf], vch[buf]]))
        for kk in range(4):
            units.append(([ka[0][p][0:64, kk * 128:(kk + 1) * 128], ka[1][p][0:64, kk * 128:(kk + 1) * 128]],
                          vtok_b[p][:, kk, :], kk, [ka[0][p], ka[1][p], vtok_b[p]]))
        if nch > 0:
            load_chunk(0)
        nu = len(units)
        acc = [(PS[0], PS[1]), (PS[2], PS[3])]

        def pv(n, m):
            kaps, vap, kk, rd = units[n]
            P_ = Pt[m][n % 2]
            S.op("pe", lambda e, vap=vap, P_=P_, m=m, n=n: e.matmul(acc[m][0][:, :], lhsT=vap, rhs=P_[:], start=(n == 0), stop=(n == nu - 1)),
                 reads=[rd[2], P_], writes=[acc[m][0]])
            S.op("pe", lambda e, P_=P_, m=m, n=n: e.matmul(acc[m][1][:, :], lhsT=onesb[:], rhs=P_[:], start=(n == 0), stop=(n == nu - 1)),
                 reads=[onesb, P_], writes=[acc[m][1]])

        for n in range(nu):
            kaps, vap, kk, rd = units[n]
            for m in range(2):
                P_ = Pt[m][n % 2]
                if kk is None:
                    qap = qa[m][p][0:68, :]
                else:
                    qap = qa[m][p][0:64, :]
                S.op("pe", lambda e, kap=kaps[m], qap=qap: e.matmul(PS[4][:, :], lhsT=kap, rhs=qap, start=True, stop=True),
                     reads=[rd[m], qa[m][p]], writes=[PS[4]])
                if kk is None:
                    S.op("act", lambda e, P_=P_: e.activation(out=P_[:], in_=PS[4][:, :], func=AF.Exp, scale=SCALE), reads=[PS[4]], writes=[P_])
                else:
                    S.op("dve", lambda e, kk=kk: e.scalar_tensor_tensor(out=tmpS[:], in0=PS[4][:, :], scalar=SCALE, in1=biasdiag[:, kk, :],
                                                                 op0=ALU.mult, op1=ALU.add), reads=[PS[4], biasdiag], writes=[tmpS])
                    S.op("act", lambda e, P_=P_: e.activation(out=P_[:], in_=tmpS[:], func=AF.Exp), reads=[tmpS], writes=[P_])
                if n > 0:
                    pv(n - 1, m)
            if n in first_of_chunk and first_of_chunk[n] + 1 < nch:
                load_chunk(first_of_chunk[n] + 1)
        for m in range(2):
            pv(nu - 1, m)
        att_epilogue(b, PS[0][:, :], PS[1][:, :], PS[2][:, :], PS[3][:, :], [PS[0], PS[1], PS[2], PS[3]])

    kc_pos_loaded = [False]

    def att_sample(b):
        p = b % 2
        if not kc_pos_loaded[0]:
            kc_pos_loaded[0] = True
            for m in range(2):
                for i in range(2):
                    S.dma("sp", kch[m][i][64:68, :], kcpos.ap(), reads=[kcpos], writes=[kch[m][i]])
        for tt in range(4):
            ts_ = slice(tt * 128, (tt + 1) * 128)
            for m in range(2):
                S.op("pe", lambda e, m=m, ts_=ts_: e.matmul(PS[4][:, m * 128:(m + 1) * 128], lhsT=ka[m][p][0:64, ts_], rhs=qa[m][p][0:64, ts_],
                                                        start=True, stop=True), reads=[ka[m][p], qa[m][p]], writes=[PS[4]])
            S.op("dve", lambda e: e.scalar_tensor_tensor(out=tmpS[:, 0:256], in0=PS[4][:, 0:256], scalar=SCALE, in1=biasnew[:], op0=ALU.mult, op1=ALU.add),
                 reads=[PS[4], biasnew], writes=[tmpS])
            S.op("act", lambda e: e.activation(out=Pn[:], in_=tmpS[:, 0:256], func=AF.Exp), reads=[tmpS], writes=[Pn])
            for m in range(2):
                S.op("pe", lambda e, m=m, tt=tt: e.matmul(PS[3][:, m * 128:(m + 1) * 128], lhsT=vtok_b[p][:, tt, :], rhs=Pn[:, m * 128:(m + 1) * 128],
                                                      start=True, stop=True), reads=[vtok_b[p], Pn], writes=[PS[3]])
                S.op("pe", lambda e, m=m: e.matmul(PS[3][:, 256 + m * 128:256 + (m + 1) * 128], lhsT=onesb[:], rhs=Pn[:, m * 128:(m + 1) * 128],
                                               start=True, stop=True), reads=[onesb, Pn], writes=[PS[3]])
            S.op("act", lambda e: e.activation(out=onsb[:], in_=PS[3][:, :], func=AF.Identity), reads=[PS[3]], writes=[onsb])
            for sg in (range(4) if KSA >= 2 else []):
                sq_ = (b - NPB) * 16 + tt * 4 + sg
                buf = sg % 2
                for hf in range(2):
                    S.dma("sp", kcf[:].rearrange("p (t d) -> p t d", t=4), ck.ap()[sq_, hf * 512:(hf + 1) * 512, :].rearrange("(t p) d -> p t d", p=128),
                          reads=[ck], writes=[kcf])
                    S.dma("sp", vcf[:].rearrange("p (t d) -> p t d", t=4), cvv.ap()[sq_, hf * 512:(hf + 1) * 512, :].rearrange("(t p) d -> p t d", p=128),
                          reads=[cvv], writes=[vcf])
                    S.op("pool", lambda e, buf=buf, hf=hf: e.tensor_copy(out=vch[buf][:, hf * 4:(hf + 1) * 4, :], in_=vcf[:].rearrange("p (t d) -> p t d", t=4)),
                         reads=[vcf], writes=[vch[buf]])
                    for m in range(2):
                        pb = gen_bank()
                        for i in range(4):
                            S.op("pe", lambda e, pb=pb, i=i, m=m: e.transpose(pb[0:64, i * 128:(i + 1) * 128], kcf[:, i * 128 + m * 64: i * 128 + (m + 1) * 64], ident_f[:]),
                                 reads=[kcf, ident_f], writes=[pb])
                        S.op("act", lambda e, pb=pb, m=m, hf=hf, buf=buf: e.activation(out=kch[m][buf][0:64, hf * 512:(hf + 1) * 512], in_=pb[0:64, :], func=AF.Identity),
                             reads=[pb], writes=[kch[m][buf]])
                if KSA < 3:
                    continue
                qs = slice(tt * 128 + sg * 32, tt * 128 + sg * 32 + 32)
                for kt in range(8):
                    for m in range(2):
                        c0 = (kt * 2 + m) * 32
                        S.op("pe", lambda e, kt=kt, m=m, c0=c0, buf=buf, qs=qs: e.matmul(PS[4][:, c0:c0 + 32], lhsT=kch[m][buf][0:68, kt * 128:(kt + 1) * 128],
                                                                               rhs=qa[m][p][0:68, qs], start=True, stop=True),
                             reads=[kch[m][buf], qa[m][p]], writes=[PS[4]])
                Pc = Pt[0][sg % 2]
                S.op("act", lambda e, Pc=Pc: e.activation(out=Pc[:], in_=PS[4][:, :], func=AF.Exp, scale=SCALE), reads=[PS[4]], writes=[Pc])
                for m in (range(2) if KSA >= 4 else []):
                    oc0 = (sg * 2 + m) * 32
                    for kt in (range(8) if KSAO != 2 else []):
                        c0 = (kt * 2 + m) * 32
                        S.op("pe", lambda e, kt=kt, c0=c0, oc0=oc0, buf=buf, Pc=Pc: e.matmul(PS[0][:, oc0:oc0 + 32], lhsT=vch[buf][:, kt, :], rhs=Pc[:, c0:c0 + 32],
                                                                                   start=(kt == 0), stop=(kt == 7)), reads=[vch[buf], Pc], writes=[PS[0]])
                    for kt in (range(8) if KSAO != 1 else []):
                        c0 = (kt * 2 + m) * 32
                        S.op("pe", lambda e, kt=kt, c0=c0, oc0=oc0, Pc=Pc: e.matmul(PS[1][:, oc0:oc0 + 32], lhsT=onesb[:], rhs=Pc[:, c0:c0 + 32],
                                                                          start=(kt == 0), stop=(kt == 7)), reads=[onesb, Pc], writes=[PS[1]])
            for m in range(2):
                for ol in range(2):
                    dst = osb[m * 2 + ol]
                    srcp = PS[ol]
                    S.op("dve", lambda e, dst=dst, srcp=srcp, m=m, ol=ol, ts_=ts_: e.tensor_tensor(
                        out=dst[:, ts_].rearrange("p (s q) -> p s q", s=4),
                        in0=srcp[:, 0:256].rearrange("p (s m q) -> p s m q", s=4, m=2)[:, :, m, :],
                        in1=onsb[:, ol * 256 + m * 128: ol * 256 + (m + 1) * 128].rearrange("p (s q) -> p s q", s=4), op=ALU.add),
                        reads=[srcp, onsb], writes=[dst])
        att_epilogue(b, osb[0][:], osb[1][:], osb[2][:], osb[3][:], [osb[0], osb[1], osb[2], osb[3]])

    blks = [int(v) for v in KBLKS.split(',')] if KBLKS else list(range(NBLK))
    blks_set = set(blks)
    loaded = set()

    def load_x(t):
        if t in loaded or t >= NTOK // 128:
            return
        loaded.add(t)
        buf = xt[t % 2]
        S.dma("sp", buf[:], x_all.ap()[t * 128:(t + 1) * 128, :], reads=[x_all], writes=[buf])

    def stage_a_block(b):
        p = b % 2
        hb = hT[p]
        sample = b >= NPB
        for tt in range(4):
            t = b * 4 + tt
            xb = xt[t % 2]
            load_x(t)
            if tt < 3 or (b + 1) in blks_set:
                load_x(t + 1)
            s_ = ss[t % 4]
            r_ = rs[t % 4]
            nb_ = xn[t % 2]
            S.op("act", lambda e, xb=xb, s_=s_: e.activation(out=junk[:], in_=xb[:], func=AF.Square, accum_out=s_[:]),
                 reads=[xb], writes=[junk, s_])
            S.op("dve", lambda e, s_=s_, r_=r_: e.tensor_scalar(out=r_[:], in0=s_[:], scalar1=1.0 / D, scalar2=EPS,
                                                                op0=ALU.mult, op1=ALU.add), reads=[s_], writes=[r_])
            S.op("act", lambda e, r_=r_: e.activation(out=r_[:], in_=r_[:], func=AF.Ln), reads=[r_], writes=[r_])
            S.op("act", lambda e, r_=r_: e.activation(out=r_[:], in_=r_[:], func=AF.Exp, scale=-0.5), reads=[r_], writes=[r_])
            S.op("dve", lambda e, xb=xb, r_=r_, nb_=nb_: e.tensor_scalar(out=nb_[:], in0=xb[:], scalar1=r_[:, 0:1],
                                                                          scalar2=None, op0=ALU.mult),
                 reads=[xb, r_], writes=[nb_])
            for grp in range(2):
                pb = gen_bank()
                pbb = pb[:].bitcast(BF16)
                for i in range(8):
                    dt_ = grp * 8 + i
                    S.op("pe", lambda e, pbb=pbb, nb_=nb_, dt_=dt_, i=i: e.transpose(
                        pbb[:, i * 128:(i + 1) * 128], nb_[:, dt_ * 128:(dt_ + 1) * 128], ident_b[:]),
                        reads=[nb_, ident_b], writes=[pb])
                for i in range(8):
                    dt_ = grp * 8 + i
                    if not sample and i % 2 == 1:
                        S.op("dve", lambda e, pbb=pbb, hb=hb, dt_=dt_, i=i, tt=tt: e.tensor_scalar(
                            out=hb[:, dt_, tt * 128:(tt + 1) * 128], in0=pbb[:, i * 128:(i + 1) * 128],
                            scalar1=ApT[0][:, dt_:dt_ + 1], scalar2=SpT[0][:, dt_:dt_ + 1], op0=ALU.mult, op1=ALU.add),
                            reads=[pb, ApT[0], SpT[0]], writes=[hb])
                    elif not sample:
                        S.op("act", lambda e, pbb=pbb, hb=hb, dt_=dt_, i=i, tt=tt: e.activation(
                            out=hb[:, dt_, tt * 128:(tt + 1) * 128], in_=pbb[:, i * 128:(i + 1) * 128],
                            func=AF.Identity, scale=ApT[0][:, dt_:dt_ + 1], bias=SpT[0][:, dt_:dt_ + 1]),
                            reads=[pb, ApT[0], SpT[0]], writes=[hb])
                    else:
                        for sq_i in range(4):
                            sidx = (t - NPB * 4) * 4 + sq_i
                            S.op("act", lambda e, pbb=pbb, hb=hb, dt_=dt_, i=i, tt=tt, sq_i=sq_i, sidx=sidx: e.activation(
                                out=hb[:, dt_, tt * 128 + sq_i * 32: tt * 128 + sq_i * 32 + 32],
                                in_=pbb[:, i * 128 + sq_i * 32: i * 128 + sq_i * 32 + 32], func=AF.Identity,
                                scale=AsT[:, dt_, sidx:sidx + 1], bias=SsT[:, dt_, sidx:sidx + 1]),
                                reads=[pb, AsT, SsT], writes=[hb])
        pb = gen_bank()
        for kt in range(16):
            S.op("pe", lambda e, pb=pb, kt=kt: e.matmul(pb[:, :], lhsT=wA_sb[:, kt, 0:128], rhs=hb[:, kt, :],
                                                        start=(kt == 0), stop=(kt == 15)), reads=[wA_sb, hb], writes=[pb])
        S.op("act", lambda e, pb=pb: e.activation(out=uT[p][:], in_=pb[:, :], func=AF.Identity), reads=[pb], writes=[uT[p]])
        if os.environ.get("KUTF", "1") == "1":
            S.op("dve", lambda e, pb=pb: e.tensor_copy(out=uTf[p][:], in_=pb[:, :]), reads=[pb], writes=[uTf[p]])
        pb = gen_bank()
        for kt in range(16):
            S.op("pe", lambda e, pb=pb, kt=kt: e.matmul(pb[:, :], lhsT=wA_sb[:, kt, 384:512], rhs=hb[:, kt, :],
                                                        start=(kt == 0), stop=(kt == 15)), reads=[wA_sb, hb], writes=[pb])
        S.op("act", lambda e, pb=pb: e.activation(out=zaT[p][:], in_=pb[:, :], func=AF.Identity), reads=[pb], writes=[zaT[p]])
        for gi, (c0, dst, isk) in enumerate(((128, qa[0][p], False), (192, qa[1][p], False),
                                              (256, ka[0][p], True), (320, ka[1][p], True))):
            pb = gen_bank()
            for kt in range(16):
                S.op("pe", lambda e, pb=pb, kt=kt, c0=c0: e.matmul(pb[0:64, :], lhsT=wA_sb[:, kt, c0:c0 + 64], rhs=hb[:, kt, :],
                                                                   start=(kt == 0), stop=(kt == 15)),
                     reads=[wA_sb, hb], writes=[pb])
            sq_ = sq[gi % 2]
            rt_ = rstd_t[gi % 2]
            S.op("act", lambda e, pb=pb, sq_=sq_: e.activation(out=sq_[:], in_=pb[0:64, :], func=AF.Square),
                 reads=[pb], writes=[sq_])
            pm = gen_bank()
            S.op("pe", lambda e, pm=pm, sq_=sq_: e.matmul(pm[0:64, :], lhsT=ones64[:], rhs=sq_[:], start=True, stop=True),
                 reads=[ones64, sq_], writes=[pm])
            S.op("act", lambda e, pm=pm, rt_=rt_: e.activation(out=rt_[:], in_=pm[0:64, :], func=AF.Ln, bias=EPS),
                 reads=[pm], writes=[rt_])
            S.op("act", lambda e, rt_=rt_: e.activation(out=rt_[:], in_=rt_[:], func=AF.Exp, scale=-0.5),
                 reads=[rt_], writes=[rt_])
            if isk:
                kf = knf[gi - 2]
                S.op("dve", lambda e, pb=pb, rt_=rt_, kf=kf, gi=gi: e.scalar_tensor_tensor(
                    out=kf[:], in0=pb[0:64, :], scalar=qkg_sb[:, gi:gi + 1], in1=rt_[:], op0=ALU.mult, op1=ALU.mult),
                    reads=[pb, rt_, qkg_sb], writes=[kf])
                S.op("pool", lambda e, kf=kf, dst=dst: e.tensor_copy(out=dst[0:64, :], in_=kf[:]), reads=[kf], writes=[dst])
            else:
                S.op("dve", lambda e, pb=pb, rt_=rt_, dst=dst, gi=gi: e.scalar_tensor_tensor(
                    out=dst[0:64, :], in0=pb[0:64, :], scalar=qkg_sb[:, gi:gi + 1], in1=rt_[:], op0=ALU.mult, op1=ALU.mult),
                    reads=[pb, rt_, qkg_sb], writes=[dst])
        pb = gen_bank()
        for tt in range(4):
            for m in range(2):
                S.op("pe", lambda e, pb=pb, tt=tt, m=m: e.transpose(
                    pb[:, tt * 128 + m * 64: tt * 128 + m * 64 + 64], knf[m][:, tt * 128:(tt + 1) * 128], ident_f[0:64, 0:64]),
                    reads=[knf[m], ident_f], writes=[pb])
        S.op("dve", lambda e, pb=pb: e.tensor_copy(out=ktok_f[p][:], in_=pb[:, :].rearrange("p (t d) -> p t d", t=4)),
             reads=[pb], writes=[ktok_f[p]])
        S.dma("pool", k_out.ap()[b * 512:(b + 1) * 512, :].rearrange("(t p) d -> p t d", p=128), ktok_f[p][:],
              reads=[ktok_f[p]], writes=[k_out])
        pb = gen_bank()
        for tt in range(4):
            for kt in range(16):
                S.op("pe", lambda e, pb=pb, tt=tt, kt=kt: e.matmul(pb[:, tt * 128:(tt + 1) * 128],
                                                                   lhsT=hb[:, kt, tt * 128:(tt + 1) * 128],
                                                                   rhs=wA_sb[:, kt, 512:640], start=(kt == 0), stop=(kt == 15)),
                     reads=[wA_sb, hb], writes=[pb])
        S.op("act", lambda e, pb=pb: e.activation(out=vtok_f[p][:], in_=pb[:, :].rearrange("p (t d) -> p t d", t=4),
                                                  func=AF.Identity), reads=[pb], writes=[vtok_f[p]])
        S.op("pool", lambda e: e.tensor_copy(out=vtok_b[p][:], in_=vtok_f[p][:]), reads=[vtok_f[p]], writes=[vtok_b[p]])
        S.dma("pool", v_out.ap()[b * 512:(b + 1) * 512, :].rearrange("(t p) d -> p t d", p=128), vtok_f[p][:],
              reads=[vtok_f[p]], writes=[v_out])

    for b in (blks if KSTAGE >= 2 else []):
        att_pos_rows(b)
        stage_a_block(b)
        if KSTAGE >= 3:
            s5_front(b)
        if KSTAGE >= 4:
            if b < NPB:
                att_store_kv(b)
                att_prompt(b)
            else:
                att_sample(b)
        if KSTAGE >= 3:
            s5_back(b)
    pb = gen_bank()
    S.op("pe", lambda e, pb=pb: e.transpose(pb[0:8, 0:128], stP[:].rearrange("p r c -> p (r c)"), ident_f[:]),
         reads=[stP, ident_f], writes=[pb])
    S.op("dve", lambda e, pb=pb: e.tensor_copy(out=stP_t[:], in_=pb[0:8, 0:128]), reads=[pb], writes=[stP_t])
    S.dma("pool", s5p_out.ap().rearrange("r c p -> (r c) p"), stP_t[:], reads=[stP_t], writes=[s5p_out])
    for ri in range(2):
        for ct in range(4):
            pb = gen_bank()
            S.op("pe", lambda e, pb=pb, ct=ct, ri=ri: e.transpose(pb[0:32, 0:128], stS[:, ct, ri, :], ident_f[:]),
                 reads=[stS, ident_f], writes=[pb])
            S.op("dve", lambda e, pb=pb, ct=ct: e.tensor_copy(out=rl[0:32, ct * 128:(ct + 1) * 128], in_=pb[0:32, 0:128]),
                 reads=[pb], writes=[rl])
        S.dma("pool", s5s_out.ap()[ri], rl[0:32, :], reads=[rl], writes=[s5s_out])
    PA.close()
    PL.close()

    if KSTAGE >= 5:
        S.collective(lambda e: e.collective_compute("AllGather", ALU.bypass, replica_groups=[list(range(NCORES))],
                                                    ins=[EX_in.ap().opt()], outs=[EX_out.ap().opt()]),
                     reads=[EX_in], writes=[EX_out])
        P3 = S.phase()
        gen_pool8 = tuple(range(8))
        wstg = [P3.sb("wstg%d" % i, [128, 2048], F32) for i in range(2)]
        wcb = [P3.sb("wcb%d" % i, [128, 2048], BF16) for i in range(2)]
        ci = 0
        for src, dst, R_, C_ in ((w_zs, wz_d, D, 1024), (w_glu, wg_d, 1024, D), (w_o0, wo0_d, D, D), (w_i1, wi1_d, D, 3 * D), (w_o1, wo1_d, D, D)):
            for kt in range(R_ // 128):
                for c0 in range(0, C_, 2048):
                    cw_ = min(2048, C_ - c0)
                    st_, cb_ = wstg[ci % 2], wcb[ci % 2]
                    S.dma("sp", st_[:, 0:cw_], src.ap()[kt * 128:(kt + 1) * 128, c0:c0 + cw_], reads=[src], writes=[st_])
                    eng = ("dve", "act")[ci % 2]
                    if eng == "act":
                        S.op("act", lambda e, st_=st_, cb_=cb_, cw_=cw_: e.activation(out=cb_[:, 0:cw_], in_=st_[:, 0:cw_], func=AF.Identity), reads=[st_], writes=[cb_])
                    else:
                        S.op(eng, lambda e, st_=st_, cb_=cb_, cw_=cw_: e.tensor_copy(out=cb_[:, 0:cw_], in_=st_[:, 0:cw_]), reads=[st_], writes=[cb_])
                    S.dma("pool", dst.ap()[kt * 128:(kt + 1) * 128, c0:c0 + cw_], cb_[:, 0:cw_], reads=[cb_], writes=[dst])
                    ci += 1
        P3.close()
        P3 = S.phase()
        cw_sb = P3.sb("cw_sb", [128, 16, 31], F32)
        cv4_sb = P3.sb("cv4_sb", [128, 16, 3], F32)
        hm_sb = P3.sb("hm_sb", [128, 1], F32)
        gidx_sb = P3.sb("gidx_sb", [128, 32], I32)
        S.dma("sp", cw_sb[:], cvw.ap(), reads=[cvw], writes=[cw_sb])
        S.dma("sp", cv4_sb[:], cvv4.ap(), reads=[cvv4], writes=[cv4_sb])
        S.dma("sp", hm_sb[:], hmask.ap(), reads=[hmask], writes=[hm_sb])
        S.dma("sp", gidx_sb[:], gidx.ap(), reads=[gidx], writes=[gidx_sb])
        xt3 = P3.sb("xt3", [128, 4, D], F32)
        hT3 = P3.sb("hT3", [128, 16, 512], BF16)
        sA = P3.sb("sA", [128, 16, 512], BF16)
        wbuf = [P3.sb("wbuf%d" % i, [128, 16, 512], BF16) for i in range(2)]
        Gg = P3.sb("Gg", [128, 8, 512], BF16)
        Ga = P3.sb("Ga", [128, 8, 512], BF16)
        gT = P3.sb("gT", [128, 16, 544], BF16)
        gS = P3.sb("gS", [128, 16, 4, 62], BF16)
        gate_sb = P3.sb("gate_sb", [128, D], F32)
        tmp3 = [P3.sb("tmp3_%d" % i, [128, 512], F32) for i in range(2)]
        cacc = [P3.sb("cacc%d" % i, [128, 512], F32) for i in range(2)]
        xn3 = P3.sb("xn3", [128, D], BF16)
        ss3 = P3.sb("ssq3", [128, 4], F32)
        mean3 = P3.sb("mean3", [128, 512], F32)
        rstd3 = P3.sb("rstd3", [128, 512], F32)
        cstf = P3.sb("cstf", [30, D], F32)
        ctok = P3.sb("ctok", [32, D], F32)
        wrr = [0]

        def panel(wd, c0, ncols, nkt, kt0=0, slot0=0, buf=None):
            if buf is None:
                buf = wbuf[wrr[0] % 2]
                wrr[0] += 1
            S.dma("sp", buf[:, slot0:slot0 + nkt, 0:ncols],
                  wd.ap()[kt0 * 128:(kt0 + nkt) * 128, c0:c0 + ncols].rearrange("(kt p) n -> p kt n", p=128),
                  reads=[wd], writes=[buf])
            return buf

        def norm_T(n, layer, kind):
            np_ = min(128, n)
            for tt in range((n + 127) // 128):
                S.op("act", lambda e, tt=tt, np_=np_: e.activation(out=xn3[0:np_, :], in_=xt3[0:np_, tt, :], func=AF.Square, accum_out=ss3[0:np_, tt:tt + 1]),
                     reads=[xt3], writes=[xn3, ss3])
                S.op("dve", lambda e, tt=tt, np_=np_: e.tensor_scalar(out=ss3[0:np_, tt:tt + 1], in0=ss3[0:np_, tt:tt + 1], scalar1=1.0 / D, scalar2=EPS, op0=ALU.mult, op1=ALU.add),
                     reads=[ss3], writes=[ss3])
                S.op("act", lambda e, tt=tt, np_=np_: e.activation(out=ss3[0:np_, tt:tt + 1], in_=ss3[0:np_, tt:tt + 1], func=AF.Ln), reads=[ss3], writes=[ss3])
                S.op("act", lambda e, tt=tt, np_=np_: e.activation(out=ss3[0:np_, tt:tt + 1], in_=ss3[0:np_, tt:tt + 1], func=AF.Exp, scale=-0.5), reads=[ss3], writes=[ss3])
                S.op("dve", lambda e, tt=tt, np_=np_: e.tensor_scalar(out=xn3[0:np_, :], in0=xt3[0:np_, tt, :], scalar1=ss3[0:np_, tt:tt + 1], scalar2=None, op0=ALU.mult),
                     reads=[xt3, ss3], writes=[xn3])
                for grp in range(2):
                    pb = gen_bank(gen_pool8)
                    pbb = pb[:].bitcast(BF16)
                    for i in range(8):
                        dt_ = grp * 8 + i
                        S.op("pe", lambda e, pbb=pbb, dt_=dt_, i=i, np_=np_: e.transpose(pbb[:, i * 128:i * 128 + np_], xn3[0:np_, dt_ * 128:(dt_ + 1) * 128], ident_b[0:np_, 0:np_]),
                             reads=[xn3, ident_b], writes=[pb])
                    for i in range(8):
                        dt_ = grp * 8 + i
                        if kind == "p":
                            S.op("act", lambda e, pbb=pbb, dt_=dt_, i=i, tt=tt, np_=np_: e.activation(
                                out=hT3[:, dt_, tt * 128:tt * 128 + np_], in_=pbb[:, i * 128:i * 128 + np_], func=AF.Identity,
                                scale=ApT[layer][:, dt_:dt_ + 1], bias=SpT[layer][:, dt_:dt_ + 1]), reads=[pb, ApT[layer], SpT[layer]], writes=[hT3])
                        else:
                            for sg in range(4):
                                S.op("act", lambda e, pbb=pbb, dt_=dt_, i=i, sg=sg: e.activation(
                                    out=hT3[:, dt_, sg * 32:(sg + 1) * 32], in_=pbb[:, i * 128 + sg * 32:i * 128 + (sg + 1) * 32], func=AF.Identity,
                                    scale=AoT[layer][:, dt_, sg:sg + 1], bias=SoT[layer][:, dt_, sg:sg + 1]), reads=[pb, AoT[layer], SoT[layer]], writes=[hT3])

        def out_proj(n, wd, lhs_fn, lhs_reads, layer, kind, final):
            np_ = min(128, n)
            S.dma("sp", gate_sb[:], gate_d.ap()[layer, 0 if kind == "p" else 1], reads=[gate_d], writes=[gate_sb])
            for nb in range(4):
                wp = panel(wd, nb * 512, 512, 16)
                for tt in range((n + 127) // 128):
                    pb = gen_bank(gen_pool8)
                    for kt in range(16):
                        S.op("pe", lambda e, pb=pb, kt=kt, tt=tt, wp=wp, np_=np_: e.matmul(pb[0:np_, :], lhsT=lhs_fn(kt, tt, np_), rhs=wp[:, kt, :],
                                                                                start=(kt == 0), stop=(kt == 15)), reads=lhs_reads + [wp], writes=[pb])
                    t3 = tmp3[(nb + tt) % 2]
                    S.op("dve", lambda e, pb=pb, t3=t3, nb=nb, np_=np_: e.tensor_tensor(out=t3[0:np_, :], in0=pb[0:np_, :], in1=gate_sb[0:np_, nb * 512:(nb + 1) * 512], op=ALU.mult),
                         reads=[pb, gate_sb], writes=[t3])
                    S.op("dve", lambda e, t3=t3, nb=nb, tt=tt, np_=np_: e.tensor_tensor(out=xt3[0:np_, tt, nb * 512:(nb + 1) * 512], in0=t3[0:np_, :],
                                                                             in1=xt3[0:np_, tt, nb * 512:(nb + 1) * 512], op=ALU.add), reads=[t3, xt3], writes=[xt3])
            if final is not None:
                for tt in range((n + 127) // 128):
                    S.dma("pool", y_out.ap()[final + tt * 128:final + tt * 128 + np_, :], xt3[0:np_, tt, :], reads=[xt3], writes=[y_out])

        def s3_block(kind, n, xsrc, excol, idxoff, final_row, conv_mode, g0=0):
            np_ = min(128, n)
            nt = (n + 127) // 128
            for tt in range(nt):
                S.dma("sp", xt3[0:np_, tt, :], xsrc[tt * 128:tt * 128 + np_, :], reads=[x_own, x_halo], writes=[xt3])
            norm_T(n, 0, kind)
            npc = (n + 127) // 128
            for j in range(8):
                for a, dst in ((0, Gg), (1, Ga)):
                    col = idxoff + j * 2 + a
                    for pi in range(npc):
                        S.dma("pool", None, None, reads=[EX_out, gidx_sb], writes=[dst],
                              fn=lambda e, dst=dst, j=j, col=col, pi=pi: e.indirect_dma_start(
                                  out=dst[:, j, pi * 128:(pi + 1) * 128], out_offset=None, in_=EX_out.ap(),
                                  in_offset=bass.IndirectOffsetOnAxis(ap=gidx_sb[:, col:col + 1], axis=0),
                                  element_offset=excol + pi * 128))
            for c in range(2):
                wp = panel(wz_d, c * 512, 512, 16)
                for i in range(4):
                    mt = c * 4 + i
                    pb = gen_bank(gen_pool8)
                    for kt in range(16):
                        S.op("pe", lambda e, pb=pb, kt=kt, i=i, wp=wp: e.matmul(pb[:, 0:n], lhsT=wp[:, kt, i * 128:(i + 1) * 128], rhs=hT3[:, kt, 0:n],
                                                                          start=(kt == 0), stop=(kt == 15)), reads=[wp, hT3], writes=[pb])
                    S.op("act", lambda e, pb=pb, mt=mt: e.activation(out=sA[:, mt, 0:n], in_=pb[:, 0:n], func=AF.Silu), reads=[pb], writes=[sA])
            for c in range(2):
                wp = panel(wg_d, c * 512, 512, 8)
                panel(wg_d, 1024 + c * 512, 512, 8, slot0=8, buf=wp)
                for i in range(4):
                    mt = c * 4 + i
                    pa_ = gen_bank(gen_pool8)
                    pb_ = gen_bank(gen_pool8)
                    for kt in range(8):
                        S.op("pe", lambda e, pa_=pa_, kt=kt, i=i, wp=wp: e.matmul(pa_[:, 0:n], lhsT=wp[:, kt, i * 128:(i + 1) * 128], rhs=Gg[:, kt, g0:g0 + n],
                                                                            start=(kt == 0), stop=(kt == 7)), reads=[wp, Gg], writes=[pa_])
                    for kt in range(8):
                        S.op("pe", lambda e, pb_=pb_, kt=kt, i=i, wp=wp: e.matmul(pb_[:, 0:n], lhsT=wp[:, 8 + kt, i * 128:(i + 1) * 128], rhs=Gg[:, kt, g0:g0 + n],
                                                                            start=(kt == 0), stop=(kt == 7)), reads=[wp, Gg], writes=[pb_])
                    t3 = tmp3[i % 2]
                    S.op("act", lambda e, pb_=pb_, t3=t3: e.activation(out=t3[:, 0:n], in_=pb_[:, 0:n], func=AF.Sigmoid), reads=[pb_], writes=[t3])
                    S.op("dve", lambda e, pa_=pa_, t3=t3: e.tensor_tensor(out=t3[:, 0:n], in0=pa_[:, 0:n], in1=t3[:, 0:n], op=ALU.mult), reads=[pa_, t3], writes=[t3])
                    S.op("dve", lambda e, t3=t3, mt=mt: e.tensor_tensor(out=sA[:, 8 + mt, 0:n], in0=t3[:, 0:n], in1=sA[:, mt, 0:n], op=ALU.mult), reads=[t3, sA], writes=[sA])
            out_proj(n, wo0_d, lambda kt, tt, np_: (sA[:, 8 + kt, tt * 128:tt * 128 + np_] if kt < 8 else Ga[:, kt - 8, g0 + tt * 128:g0 + tt * 128 + np_]),
                     [sA, Ga], 0, kind, None)
            norm_T(n, 1, kind)
            if conv_mode == "sample":
                gdst = lambda ft: gS[:, ft, :, 30:62]
                gview = lambda ap: ap.rearrange("p (s q) -> p s q", s=4)
                gt_ = gS
            else:
                if conv_mode == "next":
                    S.op("dve", lambda e: e.tensor_copy(out=gT[:, :, 0:30], in_=gT[:, :, 512:542]), reads=[gT], writes=[gT])
                gdst = lambda ft: gT[:, ft, 30:30 + n]
                gview = lambda ap: ap
                gt_ = gT
            for c in (4, 5, 6, 7, 0, 1, 2, 3, 8, 9, 10, 11):
                wp = panel(wi1_d, c * 512, 512, 16)
                for i in range(4):
                    ft = (c % 4) * 4 + i
                    pb = gen_bank(gen_pool8)
                    for kt in range(16):
                        S.op("pe", lambda e, pb=pb, kt=kt, i=i, wp=wp: e.matmul(pb[:, 0:n], lhsT=wp[:, kt, i * 128:(i + 1) * 128], rhs=hT3[:, kt, 0:n],
                                                                          start=(kt == 0), stop=(kt == 15)), reads=[wp, hT3], writes=[pb])
                    if c >= 8:
                        S.op("act", lambda e, pb=pb, ft=ft: e.activation(out=sA[:, ft, 0:n], in_=pb[:, 0:n], func=AF.Silu), reads=[pb], writes=[sA])
                    elif c >= 4:
                        S.op("act", lambda e, pb=pb, ft=ft: e.activation(out=gdst(ft), in_=gview(pb[:, 0:n]), func=AF.Sigmoid), reads=[pb], writes=[gt_])
                    else:
                        S.op("dve", lambda e, pb=pb, ft=ft: e.tensor_tensor(out=gdst(ft), in0=gview(pb[:, 0:n]), in1=gdst(ft), op=ALU.mult), reads=[pb, gt_], writes=[gt_])
            if conv_mode == "halo":
                S.op("dve", lambda e: e.tensor_scalar(out=gT[:, :, 0:30], in0=gT[:, :, 32:62], scalar1=hm_sb[:, 0:1], scalar2=None, op0=ALU.mult),
                     reads=[gT, hm_sb], writes=[gT])
                return
            ps_s = gen_bank(gen_pool8)
            ps_q = gen_bank(gen_pool8)
            for ft in range(16):
                ca = cacc[ft % 2]
                if conv_mode == "sample":
                    src = lambda w, ft=ft: gS[:, ft, :, w:w + 32]
                    cav = ca[:, 0:n].rearrange("p (s q) -> p s q", s=4)
                else:
                    src = lambda w, ft=ft: gT[:, ft, w:w + n]
                    cav = ca[:, 0:n]
                S.op("dve", lambda e, src=src, cav=cav, ft=ft: e.tensor_scalar(out=cav, in0=src(0), scalar1=cw_sb[:, ft, 0:1], scalar2=cv4_sb[:, ft, 0:1], op0=ALU.mult, op1=ALU.add),
                     reads=[gt_, cw_sb, cv4_sb], writes=[ca])
                for w in range(1, 31):
                    S.op("dve", lambda e, src=src, cav=cav, ft=ft, w=w: e.scalar_tensor_tensor(out=cav, in0=src(w), scalar=cw_sb[:, ft, w:w + 1], in1=cav, op0=ALU.mult, op1=ALU.add),
                         reads=[gt_, cw_sb, ca], writes=[ca])
                S.op("act", lambda e, ca=ca, ft=ft: e.activation(out=hT3[:, ft, 0:n], in_=ca[:, 0:n], func=AF.Identity), reads=[ca], writes=[hT3])
                t3 = tmp3[ft % 2]
                S.op("act", lambda e, ca=ca, t3=t3: e.activation(out=t3[:, 0:n], in_=ca[:, 0:n], func=AF.Square), reads=[ca], writes=[t3])
                S.op("pe", lambda e, ps_s=ps_s, ca=ca, ft=ft: e.matmul(ps_s[:, 0:n], lhsT=ones128f[:], rhs=ca[:, 0:n], start=(ft == 0), stop=(ft == 15)), reads=[ones128f, ca], writes=[ps_s])
                S.op("pe", lambda e, ps_q=ps_q, t3=t3, ft=ft: e.matmul(ps_q[:, 0:n], lhsT=ones128f[:], rhs=t3[:, 0:n], start=(ft == 0), stop=(ft == 15)), reads=[ones128f, t3], writes=[ps_q])
            S.op("act", lambda e, ps_s=ps_s: e.activation(out=mean3[:, 0:n], in_=ps_s[:, 0:n], func=AF.Identity, scale=1.0 / 16.0), reads=[ps_s], writes=[mean3])
            S.op("dve", lambda e: e.tensor_tensor(out=rstd3[:, 0:n], in0=mean3[:, 0:n], in1=mean3[:, 0:n], op=ALU.mult), reads=[mean3], writes=[rstd3])
            S.op("dve", lambda e, ps_q=ps_q: e.scalar_tensor_tensor(out=rstd3[:, 0:n], in0=ps_q[:, 0:n], scalar=1.0 / 16.0, in1=rstd3[:, 0:n], op0=ALU.mult, op1=ALU.subtract),
                 reads=[ps_q, rstd3], writes=[rstd3])
            S.op("act", lambda e: e.activation(out=rstd3[:, 0:n], in_=rstd3[:, 0:n], func=AF.Ln, bias=EPS), reads=[rstd3], writes=[rstd3])
            S.op("act", lambda e: e.activation(out=rstd3[:, 0:n], in_=rstd3[:, 0:n], func=AF.Exp, scale=-0.5), reads=[rstd3], writes=[rstd3])
            for ft in range(16):
                t3 = tmp3[ft % 2]
                S.op("dve", lambda e, t3=t3, ft=ft: e.tensor_tensor(out=t3[:, 0:n], in0=hT3[:, ft, 0:n], in1=mean3[:, 0:n], op=ALU.subtract), reads=[hT3, mean3], writes=[t3])
                S.op("dve", lambda e, t3=t3: e.tensor_tensor(out=t3[:, 0:n], in0=t3[:, 0:n], in1=rstd3[:, 0:n], op=ALU.mult), reads=[t3, rstd3], writes=[t3])
                S.op("act", lambda e, t3=t3, ft=ft: e.activation(out=t3[:, 0:n], in_=t3[:, 0:n], func=AF.Silu, scale=cv4_sb[:, ft, 1:2], bias=cv4_sb[:, ft, 2:3]),
                     reads=[t3, cv4_sb], writes=[t3])
                S.op("dve", lambda e, t3=t3, ft=ft: e.tensor_tensor(out=hT3[:, ft, 0:n], in0=t3[:, 0:n], in1=sA[:, ft, 0:n], op=ALU.mult), reads=[t3, sA], writes=[hT3])
            out_proj(n, wo1_d, lambda kt, tt, np_: hT3[:, kt, tt * 128:tt * 128 + np_], [hT3], 1, kind, final_row)

        def conv_state_out(src_fn, dst_ap):
            for grp in range(2):
                pb = gen_bank(gen_pool8)
                pbb = pb[:].bitcast(BF16)
                for i in range(8):
                    ft = grp * 8 + i
                    S.op("pe", lambda e, pbb=pbb, i=i, ft=ft: e.transpose(pbb[0:30, i * 128:(i + 1) * 128], src_fn(ft), ident_b[:]),
                         reads=[gT, gS, ident_b], writes=[pb])
                S.op("act", lambda e, pbb=pbb, grp=grp: e.activation(out=ctok[0:30, grp * 1024:(grp + 1) * 1024], in_=pbb[0:30, :], func=AF.Identity), reads=[pb], writes=[ctok])
            S.dma("pool", dst_ap, ctok[0:30, :], reads=[ctok], writes=[convp_out, convs_out])

        s3_block("p", 32, x_halo.ap(), 1920, 16, None, "halo", g0=96)
        for pb_i in range(4):
            s3_block("p", 512, x_own.ap()[pb_i * 512:(pb_i + 1) * 512, :], pb_i * 512, 0, pb_i * 512, "first" if pb_i == 0 else "next")
        conv_state_out(lambda ft: gT[:, ft, 512:542], convp_out.ap())
        for sg in range(4):
            S.dma("sp", cstf[:], cst.ap()[sg], reads=[cst], writes=[cstf])
            for grp in range(2):
                pb = gen_bank(gen_pool8)
                for i in range(4):
                    pass
            for ft in range(16):
                pb = gen_bank(gen_pool8)
                S.op("pe", lambda e, pb=pb, ft=ft: e.transpose(pb[:, 0:30], cstf[0:30, ft * 128:(ft + 1) * 128], ident_f[0:30, 0:30]), reads=[cstf, ident_f], writes=[pb])
                S.op("act", lambda e, pb=pb, ft=ft, sg=sg: e.activation(out=gS[:, ft, sg, 0:30], in_=pb[:, 0:30], func=AF.Identity), reads=[pb], writes=[gS])
        s3_block("o", 128, x_own.ap()[2048:2176, :], 2048, 0, 2048, "sample")
        for sg in range(4):
            conv_state_out(lambda ft, sg=sg: gS[:, ft, sg, 32:62], convs_out.ap()[sg])
        P3.close()


    S.finish()
    S.emit()
    return nc


_CACHE = {}


def kernel(x_prompt, x_sample, c_prompt, c_sample, cache_k, cache_v, state_s5_re, state_s5_im,
           state_conv, norm_g, w_ada, b_ada, w_in_even, w_out_even, s5_lam_re, s5_lam_im,
           s5_log_dt, s5_b_re, s5_b_im, s5_c_re, s5_c_im, s5_d, s5_w_glu, q_norm_g, k_norm_g,
           lam_q1, lam_k1, lam_q2, lam_k2, attn_out_g, w_in_odd, conv_w, conv_b, conv_ln_g,
           conv_ln_b, w_out_odd):
    f32 = np.float32
    A = lambda a: np.ascontiguousarray(np.asarray(a, dtype=f32))
    x_all = np.concatenate([A(x_prompt).reshape(SEQ, D), A(x_sample).reshape(NSEQ * DSEQ, D)], axis=0)
    w_in_even = A(w_in_even)[0]
    sel = np.zeros((68, 128), f32)
    sel[32, :] = 1.0
    sel[64 + np.arange(128) // 32, np.arange(128)] = 1.0
    in_maps = []
    for j in range(NCORES):
        c_all = np.zeros((68, D), f32)
        c_all[0:32] = A(c_sample)
        c_all[32] = A(c_prompt)[0]
        c_all[64:68] = A(c_sample)[4 * j:4 * j + 4]
        h0 = 128 * j
        cols = np.concatenate([np.arange(h0, h0 + 128),
                               2048 + h0 + np.arange(128),
                               3072 + h0 + np.arange(128),
                               5120 + h0 + np.arange(128),
                               4096 + h0 + np.arange(128)])
        wA = np.ascontiguousarray(w_in_even[:, cols])
        qg = A(q_norm_g)[0]
        kg = A(k_norm_g)[0]
        qkg = np.ascontiguousarray(np.stack([qg[:64], qg[64:], kg[:64], kg[64:]], axis=1))
        lre, lim, ldt = A(s5_lam_re)[0], A(s5_lam_im)[0], A(s5_log_dt)[0]
        bre, bim, cre, cim, dsk = A(s5_b_re)[0], A(s5_b_im)[0], A(s5_c_re)[0], A(s5_c_im)[0], A(s5_d)[0]
        s5pp = np.zeros((128, 4, 3), f32)
        s5bc = np.zeros((3, 512), f32)
        brT = np.zeros((128, 4, 128), f32); biT = np.zeros((128, 4, 128), f32)
        crT = np.zeros((128, 4, 128), f32); ciT = np.zeros((128, 4, 128), f32)
        for ct in range(4):
            for g2 in range(2):
                gl = 2 * ct + g2
                g = 8 * j + gl
                ps_ = slice(g2 * 64, g2 * 64 + 64)
                s5pp[ps_, ct, 0] = lre[g]; s5pp[ps_, ct, 1] = lim[g]; s5pp[ps_, ct, 2] = ldt[g]
                cs_ = slice(ct * 128 + g2 * 64, ct * 128 + g2 * 64 + 64)
                s5bc[0, cs_] = lre[g]; s5bc[1, cs_] = lim[g]; s5bc[2, cs_] = ldt[g]
                fs_ = slice(gl * 16, gl * 16 + 16)
                brT[fs_, ct, ps_] = bre[g].T
                biT[fs_, ct, ps_] = bim[g].T
                crT[ps_, ct, fs_] = cre[g].T
                ciT[ps_, ct, fs_] = cim[g].T
        dpp = np.ascontiguousarray(dsk[8 * j:8 * j + 8].reshape(128, 1))
        st_re = np.ascontiguousarray(A(state_s5_re)[0][:, 8 * j:8 * j + 8, :].reshape(32, 512))
        st_im = np.ascontiguousarray(A(state_s5_im)[0][:, 8 * j:8 * j + 8, :].reshape(32, 512))
        slope = 2.0 ** (-(j + 1))
        cfac = slope / SCALE
        bf = ml_dtypes.bfloat16
        pos = np.arange(NTOK)
        kq = np.zeros((8, NTOK), f32)
        kq[0, :SEQ] = cfac * 128.0 * (pos[:SEQ] // 128)
        kq[1, :SEQ] = cfac * (pos[:SEQ] % 128)
        kq[2, :SEQ] = 1.0
        kq[3, :SEQ] = 1.0
        qpos = np.concatenate([pos[:SEQ], PAST + (np.arange(NSEQ * DSEQ) % DSEQ)])
        kq[4] = 1.0
        kq[5] = 1.0
        kq[6] = -cfac * 128.0 * (qpos // 128)
        kq[7] = -cfac * (qpos % 128)
        kcp = np.zeros((4, 1024), f32)
        kcp[0] = cfac * 128.0 * (np.arange(1024) // 128)
        kcp[1] = cfac * (np.arange(1024) % 128)
        kcp[2] = 1.0
        kcp[3] = 1.0
        kk_ = np.arange(128)[:, None, None] + 128 * np.arange(4)[None, :, None]
        qq_ = np.arange(512)[None, None, :]
        vis = (kk_ // 64) <= (qq_ // 64)
        bdiag = np.where(vis, -slope * np.abs(qq_ - kk_), NEG).astype(f32)
        kn = np.arange(128)[:, None]
        qn = np.arange(128)[None, :]
        bnew1 = np.where((kn // 32) == (qn // 32), -slope * np.abs((qn % 32) - (kn % 32)), NEG).astype(f32)
        bnew = np.concatenate([bnew1, bnew1], axis=1)
        lamv = np.stack([A(lam_q1)[0], A(lam_k1)[0], A(lam_q2)[0], A(lam_k2)[0]], axis=0)
        aog = np.ascontiguousarray(A(attn_out_g)[0].reshape(128, 1))
        ck_ = np.ascontiguousarray(A(cache_k)[0][:, :, j, :])
        cv_ = np.ascontiguousarray(A(cache_v)[0][:, :, j, :])
        xp_ = x_all[:SEQ]
        x_own = np.concatenate([xp_[2048 * j:2048 * (j + 1)], x_all[SEQ + 128 * j:SEQ + 128 * (j + 1)]], axis=0)
        x_halo = xp_[2048 * j - 32:2048 * j] if j > 0 else np.zeros((32, D), f32)
        hmask = np.full((128, 1), 0.0 if j == 0 else 1.0, f32)
        EXR = 2 * 128 * 8
        gidx = np.zeros((128, 32), np.int32)
        rp = max(j - 1, 0)
        for jj in range(8):
            for a in range(2):
                gidx[:, jj * 2 + a] = (jj * EXR + (a * 128 + np.arange(128)) * 8 + j) * 17
                gidx[:, 16 + jj * 2 + a] = (jj * EXR + (a * 128 + np.arange(128)) * 8 + rp) * 17
        cw_ = A(conv_w)[0]
        cvw = np.ascontiguousarray(cw_.T.reshape(16, 128, 31).transpose(1, 0, 2))
        cvv4 = np.ascontiguousarray(np.stack([A(conv_b)[0], A(conv_ln_g)[0], A(conv_ln_b)[0]], axis=1).reshape(16, 128, 3).transpose(1, 0, 2))
        cst = np.ascontiguousarray(A(state_conv)[0][4 * j:4 * j + 4])
        in_maps.append({
            "x_own": np.ascontiguousarray(x_own), "x_halo": np.ascontiguousarray(x_halo), "hmask": hmask, "gidx": gidx,
            "w_zs": np.ascontiguousarray(w_in_even[:, 1024:2048]), "w_glu": A(s5_w_glu)[0], "w_o0": A(w_out_even)[0],
            "w_i1": A(w_in_odd)[0], "w_o1": A(w_out_odd)[0], "cvw": cvw, "cvv4": cvv4, "cst": cst,
            "kqpos": kq.astype(bf), "kcpos": kcp.astype(bf), "biasdiag": bdiag, "biasnew": bnew, "lamv": lamv, "aog": aog,
            "ck": ck_, "cvv": cv_,
            "s5pp": s5pp, "s5bc": s5bc, "brT": brT, "biT": biT, "crT": crT, "ciT": ciT, "dpp": dpp,
            "st_re": st_re, "st_im": st_im,
            "x_all": x_all, "c_all": c_all, "w_ada": A(w_ada), "b_ada": A(b_ada), "norm_g": A(norm_g),
            "wA": wA, "qkg": qkg, "sel": sel,
        })
    if "nc" not in _CACHE:
        _CACHE["nc"] = build_program()
    nc = _CACHE["nc"]
    res = run_bass_kernel_spmd(nc, in_maps, core_ids=list(range(NCORES)))
    R = res.results
    k_all = np.stack([np.asarray(R[j]["k_out"]) for j in range(NCORES)], axis=1)
    v_all = np.stack([np.asarray(R[j]["v_out"]) for j in range(NCORES)], axis=1)
    k_prompt = k_all[:SEQ].reshape(1, 1, SEQ, 8, 128)
    v_prompt = v_all[:SEQ].reshape(1, 1, SEQ, 8, 128)
    k_sample = k_all[SEQ:].reshape(1, NSEQ, DSEQ, 8, 128)
    v_sample = v_all[SEQ:].reshape(1, NSEQ, DSEQ, 8, 128)
    sp = np.stack([np.asarray(R[j]["s5p_out"]).reshape(2, 8, 64) for j in range(NCORES)], axis=1).reshape(2, 64, 64)
    ss_ = np.stack([np.asarray(R[j]["s5s_out"]).reshape(2, 32, 8, 64) for j in range(NCORES)], axis=2).reshape(2, 32, 64, 64)
    _CACHE["R"] = R
    z = lambda *s: np.zeros(s, f32)
    if KSTAGE >= 5:
        yo = [np.asarray(R[j]["y_out"]) for j in range(NCORES)]
        y_prompt = np.concatenate([y[:2048] for y in yo], axis=0).reshape(1, SEQ, D)
        y_sample = np.concatenate([y[2048:] for y in yo], axis=0).reshape(NSEQ, DSEQ, D)
        conv_p = np.asarray(R[NCORES - 1]["convp_out"]).reshape(1, 1, 30, D)
        conv_s = np.concatenate([np.asarray(R[j]["convs_out"]) for j in range(NCORES)], axis=0).reshape(1, NSEQ, 30, D)
    else:
        y_prompt, y_sample, conv_p, conv_s = z(1, SEQ, D), z(NSEQ, DSEQ, D), z(1, 1, 30, D), z(1, NSEQ, 30, D)
    return (y_prompt, y_sample, k_prompt, v_prompt, sp[0].reshape(1, 1, 64, 64), sp[1].reshape(1, 1, 64, 64), conv_p,
            k_sample, v_sample, ss_[0].reshape(1, NSEQ, 64, 64), ss_[1].reshape(1, NSEQ, 64, 64), conv_s)
```
